# Optimizing a Trainium2 kernel written in Bass

```python
import math
import jax, jax.numpy as jnp
from jax import lax
import numpy as np

D_MODEL = 1024
BATCH = 16
SEQ = 2048
DEPTH = 4

N_MIXERS = 3
N_A_LAYERS = (DEPTH + 2) // 3
N_B_LAYERS = (DEPTH + 1) // 3
N_C_LAYERS = DEPTH // 3

RMS_EPS = 1e-6
CONV_W = 3

NA_HEAD_DIM = 64
NA_HEADS = D_MODEL // NA_HEAD_DIM
GRID_W = 64
NA_MAX_ROWS = 8
NA_COLS = 16
NA_QCOLS = 16
NA_SPAN = NA_QCOLS + NA_COLS

HYENA_ORDER = 2
HYENA_BANDS = 16
HYENA_EMB = 1 + 2 * HYENA_BANDS
HYENA_FILTER_HIDDEN = 64
HYENA_DECAY_TARGET = 1e-2
HYENA_FAST_DECAY = 0.3
HYENA_SLOW_DECAY = 1.5

FFN_HIDDEN = ((8 * D_MODEL + 3 * 256 - 1) // (3 * 256)) * 256

kernel_name = 'hybrid_shortconv_natten_hyena_encoder'


def rmsnorm(x, g):
    xf = x.astype(jnp.float32)
    y = xf * lax.rsqrt(jnp.mean(xf * xf, axis=-1, keepdims=True) + RMS_EPS)
    return (y * g.astype(jnp.float32)).astype(x.dtype)


def depthwise_conv3(x, w, b=None):
    c = x.shape[-1]
    y = lax.conv_general_dilated(
        x, w.astype(x.dtype)[:, None, :], (1,), ((CONV_W // 2, CONV_W // 2),),
        dimension_numbers=('NWC', 'WIO', 'NWC'), feature_group_count=c)
    return y if b is None else y + b.astype(x.dtype)


def short_conv_mixer(h, w_in, conv_w, w_out):
    b_gate, c_gate, u = jnp.split(h @ w_in, 3, axis=-1)
    return (b_gate * depthwise_conv3(c_gate * u, conv_w)) @ w_out


def na_column_tables():
    cb = np.arange(GRID_W // NA_QCOLS)
    starts = np.clip(cb * NA_QCOLS - NA_COLS // 2, 0, GRID_W - NA_SPAN)
    key_col = starts[:, None] + np.arange(NA_SPAN)
    q_col = cb[:, None] * NA_QCOLS + np.arange(NA_QCOLS)
    col_start = np.clip(q_col - NA_COLS // 2, 0, GRID_W - NA_COLS)
    kc = key_col[:, None, :]
    valid = (kc >= col_start[:, :, None]) & (kc < col_start[:, :, None] + NA_COLS)
    dc_idx = np.clip(kc - q_col[:, :, None] + NA_COLS - 1, 0, 2 * NA_COLS - 2)
    return key_col, valid, dc_idx


def neighborhood_attention(h, w_qkv, q_g, k_g, rpb, w_out):
    bsz, seq, d = h.shape
    rows = seq // GRID_W
    wr = min(NA_MAX_ROWS, rows)
    q, k, v = jnp.split(h @ w_qkv, 3, axis=-1)
    grid = (bsz, rows, GRID_W, NA_HEADS, NA_HEAD_DIM)
    q = rmsnorm(q.reshape(grid), q_g)
    k = rmsnorm(k.reshape(grid), k_g)
    v = v.reshape(grid)
    key_col, valid, dc_idx = na_column_tables()
    ncb = key_col.shape[0]
    scale = NA_HEAD_DIM ** -0.5
    neg = jnp.finfo(jnp.float32).min
    mask = valid[None, None, :, :, None, :]

    def row_block(r):
        rs = jnp.clip(r - wr // 2, 0, rows - wr)
        q_r = lax.dynamic_index_in_dim(q, r, axis=1, keepdims=False)
        q_r = q_r.reshape(bsz, ncb, NA_QCOLS, NA_HEADS, NA_HEAD_DIM)
        k_r = lax.dynamic_slice_in_dim(k, rs, wr, axis=1)[:, :, key_col]
        v_r = lax.dynamic_slice_in_dim(v, rs, wr, axis=1)[:, :, key_col]
        s = jnp.einsum('bnqhd,brnkhd->bhnqrk', q_r, k_r).astype(jnp.float32) * scale
        dr_idx = rs + jnp.arange(wr) - r + NA_MAX_ROWS - 1
        bias = rpb[:, dr_idx[None, None, :, None], dc_idx[:, :, None, :]]
        s = jnp.where(mask, s + bias.astype(jnp.float32)[None], neg)
        p = jax.nn.softmax(s.reshape(s.shape[:4] + (wr * NA_SPAN,)), axis=-1)
        p = p.reshape(s.shape).astype(v.dtype)
        o = jnp.einsum('bhnqrk,brnkhd->bnqhd', p, v_r)
        return o.reshape(bsz, GRID_W, d)

    o = lax.map(row_block, jnp.arange(rows))
    return o.transpose(1, 0, 2, 3).reshape(bsz, seq, d) @ w_out


def hyena_filters(seq, w1, b1, w2, b2, w3, freq):
    f32 = jnp.float32
    t = jnp.linspace(0.0, 1.0, seq, dtype=f32)[:, None]
    bands = jnp.linspace(1e-4, HYENA_BANDS - 1, HYENA_BANDS, dtype=f32)
    ang = (2.0 * math.pi) * jnp.arange(seq, dtype=f32)[:, None] / seq * bands
    z = jnp.concatenate([t, jnp.cos(ang), -jnp.sin(ang)], axis=-1)
    fr = freq.astype(f32)
    hid = jnp.sin(fr * (z @ w1.astype(f32) + b1.astype(f32)))
    hid = jnp.sin(fr * (hid @ w2.astype(f32) + b2.astype(f32)))
    filt = (hid @ w3.astype(f32)).reshape(seq, HYENA_ORDER, 2, D_MODEL)
    lt = math.log(HYENA_DECAY_TARGET)
    deltas = jnp.abs(jnp.linspace(lt / HYENA_SLOW_DECAY, lt / HYENA_FAST_DECAY, D_MODEL, dtype=f32))
    filt = filt * jnp.exp(-t * deltas)[:, None, None, :]
    fwd, rev = filt[:, :, 0], filt[:, :, 1]
    h_full = jnp.concatenate([fwd[:1] + rev[:1], fwd[1:], jnp.zeros_like(fwd[:1]), rev[:0:-1]], axis=0)
    h_full = h_full / jnp.sum(jnp.abs(h_full), axis=0, keepdims=True)
    return jnp.fft.rfft(h_full, axis=0)


def hyena_mixer(h, w_in, short_w, short_b, f_w1, f_b1, f_w2, f_b2, f_w3, f_freq, f_skip, w_out):
    seq = h.shape[1]
    v, x1, x2 = jnp.split(depthwise_conv3(h @ w_in, short_w, short_b), 3, axis=-1)
    h_freq = hyena_filters(seq, f_w1, f_b1, f_w2, f_b2, f_w3, f_freq)

    def long_conv(u, n):
        uf = u.astype(jnp.float32)
        y = jnp.fft.irfft(jnp.fft.rfft(uf, n=2 * seq, axis=1) * h_freq[:, n], n=2 * seq, axis=1)[:, :seq]
        return (y + uf * f_skip[n].astype(jnp.float32)).astype(u.dtype)

    z = x1 * long_conv(v, 0)
    z = x2 * long_conv(z, 1)
    return z @ w_out


def swiglu(h, w13, w2):
    g, u = jnp.split(h @ w13, 2, axis=-1)
    return (jax.nn.silu(g) * u) @ w2


def setup_inputs(seed: int = 0) -> dict:
    key = jax.random.key(seed)
    ks = jax.random.split(key, 24)
    f32 = jnp.float32

    def nrm(k, shape, fan_in):
        return jax.random.normal(k, shape, f32) * (fan_in ** -0.5)

    def gain(k, shape):
        return 1.0 + 0.02 * jax.random.normal(k, shape, f32)

    def small(k, shape, s=0.02):
        return s * jax.random.normal(k, shape, f32)

    D = D_MODEL
    return {
        'x': jax.random.normal(ks[0], (BATCH, SEQ, D), f32),
        'norm_mix_g': gain(ks[1], (DEPTH, D)),
        'norm_ffn_g': gain(ks[2], (DEPTH, D)),
        'a_w_in': nrm(ks[3], (N_A_LAYERS, D, 3 * D), D),
        'a_conv_w': nrm(ks[4], (N_A_LAYERS, CONV_W, D), CONV_W),
        'a_w_out': nrm(ks[5], (N_A_LAYERS, D, D), D),
        'b_w_qkv': nrm(ks[6], (N_B_LAYERS, D, 3 * D), D),
        'b_q_norm_g': gain(ks[7], (N_B_LAYERS, NA_HEAD_DIM)),
        'b_k_norm_g': gain(ks[8], (N_B_LAYERS, NA_HEAD_DIM)),
        'b_rpb': small(ks[9], (N_B_LAYERS, NA_HEADS, 2 * NA_MAX_ROWS - 1, 2 * NA_COLS - 1)),
        'b_w_out': nrm(ks[10], (N_B_LAYERS, D, D), D),
        'c_w_in': nrm(ks[11], (N_C_LAYERS, D, 3 * D), D),
        'c_short_w': nrm(ks[12], (N_C_LAYERS, CONV_W, 3 * D), CONV_W),
        'c_short_b': small(ks[13], (N_C_LAYERS, 3 * D)),
        'c_f_w1': nrm(ks[14], (N_C_LAYERS, HYENA_EMB, HYENA_FILTER_HIDDEN), HYENA_EMB),
        'c_f_b1': small(ks[15], (N_C_LAYERS, HYENA_FILTER_HIDDEN)),
        'c_f_w2': nrm(ks[16], (N_C_LAYERS, HYENA_FILTER_HIDDEN, HYENA_FILTER_HIDDEN), HYENA_FILTER_HIDDEN),
        'c_f_b2': small(ks[17], (N_C_LAYERS, HYENA_FILTER_HIDDEN)),
        'c_f_w3': nrm(ks[18], (N_C_LAYERS, HYENA_FILTER_HIDDEN, HYENA_ORDER * 2 * D), HYENA_FILTER_HIDDEN),
        'c_f_freq': gain(ks[19], (N_C_LAYERS, HYENA_FILTER_HIDDEN)),
        'c_f_skip': small(ks[20], (N_C_LAYERS, HYENA_ORDER, D), 0.5),
        'c_w_out': nrm(ks[21], (N_C_LAYERS, D, D), D),
        'f_w13': nrm(ks[22], (DEPTH, D, 2 * FFN_HIDDEN), D),
        'f_w2': nrm(ks[23], (DEPTH, FFN_HIDDEN, D), FFN_HIDDEN),
    }


def reference(x, norm_mix_g, norm_ffn_g, a_w_in, a_conv_w, a_w_out,
              b_w_qkv, b_q_norm_g, b_k_norm_g, b_rpb, b_w_out,
              c_w_in, c_short_w, c_short_b, c_f_w1, c_f_b1, c_f_w2, c_f_b2,
              c_f_w3, c_f_freq, c_f_skip, c_w_out, f_w13, f_w2):
    ia = ib = ic = 0
    for i in range(DEPTH):
        h = rmsnorm(x, norm_mix_g[i])
        kind = i % N_MIXERS
        if kind == 0:
            y = short_conv_mixer(h, a_w_in[ia], a_conv_w[ia], a_w_out[ia])
            ia += 1
        elif kind == 1:
            y = neighborhood_attention(h, b_w_qkv[ib], b_q_norm_g[ib], b_k_norm_g[ib], b_rpb[ib], b_w_out[ib])
            ib += 1
        else:
            y = hyena_mixer(h, c_w_in[ic], c_short_w[ic], c_short_b[ic], c_f_w1[ic], c_f_b1[ic],
                            c_f_w2[ic], c_f_b2[ic], c_f_w3[ic], c_f_freq[ic], c_f_skip[ic], c_w_out[ic])
            ic += 1
        x = x + y
        x = x + swiglu(rmsnorm(x, norm_ffn_g[i]), f_w13[i], f_w2[i])
    return x
```

```python
import math
import os
from contextlib import ExitStack
import numpy as np
import ml_dtypes
import concourse.bass as bass
import concourse.mybir as mybir
from concourse.bass_utils import run_bass_kernel_spmd

F32 = mybir.dt.float32
BF16 = mybir.dt.bfloat16
AF = mybir.ActivationFunctionType
ALU = mybir.AluOpType

D = 1024
L = 2048
NK = 8
NTT = 4
TT = 512
DEPTH = 4
FFH = 2816
NHC = 22
EPS = 1e-6
N_CORES = 8
SEQ_PER_CORE = 2

COMPUTE = ("pe", "act", "dve", "pool")
QUEUES = ("sp",)
N_STREAMS = 8
SAME_ENG_SYNC = bool(int(os.environ.get("SES", "0")))


class Prog:
    def __init__(self):
        self.ops = {e: [] for e in COMPUTE + QUEUES}
        self.last_write = {}
        self.reads_since = {}
        self.stream_rr = {"sp": 0, "pool": 0, "act": 0}
        self.stream_cnt = {}
        self.pending = {}

    def _deps(self, me, reads, writes):
        deps = set()
        for r in reads:
            if r in self.last_write:
                deps.add(self.last_write[r])
        for w in writes:
            if w in self.last_write:
                deps.add(self.last_write[w])
            for x in self.reads_since.get(w, ()):
                deps.add(x)
        for w in writes:
            self.reads_since[w] = []
            self.last_write[w] = me
        for r in reads:
            if r not in writes:
                self.reads_since.setdefault(r, []).append(me)
        deps.discard(me)
        return deps

    def op(self, eng, fn, reads=(), writes=()):
        idx = len(self.ops[eng])
        me = ("e", eng, idx)
        deps = self._deps(me, tuple(reads), tuple(writes))
        deps |= self.pending.pop(eng, set())
        self.ops[eng].append(dict(kind="op", fn=fn, deps=deps))
        return me

    def dma(self, issuer, out, in_, reads=(), writes=()):
        s = self.stream_rr[issuer]
        self.stream_rr[issuer] = (s + 1) % N_STREAMS
        key = (issuer, s)
        k = self.stream_cnt.get(key, 0)
        self.stream_cnt[key] = k + 1
        me = ("d", issuer, s, k)
        deps = self._deps(me, tuple(reads), tuple(writes))
        if k > 0:
            deps.add(("d", issuer, s, k - 1))
        deps |= self.pending.pop(issuer, set())
        self.ops[issuer].append(dict(kind="dma", out=out, in_=in_, deps=deps, stream=s))
        return me

    def barrier(self, engines=("pe", "act", "dve"), extra=()):
        front = {}
        for e in engines:
            j = len(self.ops[e]) - 1
            while j >= 0 and self.ops[e][j]["kind"] != "op":
                j -= 1
            if j >= 0:
                front[e] = ("e", e, j)
        for e in tuple(engines) + ("sp",) + tuple(extra):
            for e2, d in front.items():
                if e2 != e:
                    self.pending.setdefault(e, set()).add(d)

    def _skip(self, e, d):
        return d[1] == e and (e == "pe" or e in QUEUES or not SAME_ENG_SYNC)

    def emit(self, nc, stack):
        engs = COMPUTE + QUEUES
        signal = {e: set() for e in engs}
        for e in engs:
            for o in self.ops[e]:
                for d in o["deps"]:
                    if d[0] == "e" and not self._skip(e, d):
                        signal[d[1]].add(d[2])
        count = {}
        for e in engs:
            c = 0
            for i, o in enumerate(self.ops[e]):
                if o["kind"] == "op" and i in signal[e]:
                    c += 1
                    count[(e, i)] = c
        sem = {e: stack.enter_context(nc.semaphore("s_" + e)) for e in engs}
        dsem = {}
        for (issuer, s) in self.stream_cnt:
            dsem[(issuer, s)] = stack.enter_context(nc.semaphore("d_%s%d" % (issuer, s)))
        block = stack.enter_context(nc.Block())
        prog = self

        def run(e, engine):
            waited = {}
            for i, o in enumerate(prog.ops[e]):
                for d in sorted(o["deps"], key=str):
                    if d[0] == "e":
                        if prog._skip(e, d):
                            continue
                        sm, val = sem[d[1]], count[(d[1], d[2])]
                        k = ("e", d[1])
                    else:
                        sm, val = dsem[(d[1], d[2])], 16 * (d[3] + 1)
                        k = ("d", d[1], d[2])
                    if waited.get(k, 0) >= val:
                        continue
                    waited[k] = val
                    engine.wait_ge(sm, val)
                if o["kind"] == "dma":
                    ins = engine.dma_start(out=o["out"], in_=o["in_"])
                    ins.then_inc(dsem[(e, o["stream"])], 16)
                else:
                    ins = o["fn"](engine)
                    if (e, i) in count:
                        ins.then_inc(sem[e], 1)

        def finish(engine):
            for (issuer, st), n in prog.stream_cnt.items():
                engine.wait_ge(dsem[(issuer, st)], 16 * n)

        @block.tensor
        def _(eng):
            run("pe", eng)

        @block.scalar
        def _(eng):
            run("act", eng)

        @block.vector
        def _(eng):
            run("dve", eng)

        @block.gpsimd
        def _(eng):
            run("pool", eng)

        @block.sync
        def _(eng):
            run("sp", eng)
            finish(eng)


def _units(W, col_lists):
    Wr = W.reshape(NK, 128, W.shape[1])
    out = []
    for cols in col_lists:
        out.append(np.ascontiguousarray(Wr[:, :, cols].transpose(1, 0, 2)).reshape(128, -1))
    return np.stack(out).astype(np.float32)


def _chunk_cols(*starts):
    return np.concatenate([np.arange(s, s + 128) for s in starts])


def _pvec(v):
    return np.ascontiguousarray(v.reshape(-1, 128).T).astype(np.float32)


def prep_shared(inp):
    sh = {}
    g = np.zeros((128, DEPTH * 2 * NK), np.float32)
    for l in range(DEPTH):
        g[:, (l * 2) * NK:(l * 2 + 1) * NK] = _pvec(inp["norm_mix_g"][l])
        g[:, (l * 2 + 1) * NK:(l * 2 + 2) * NK] = _pvec(inp["norm_ffn_g"][l])
    sh["gvec"] = g
    for l in range(DEPTH):
        w13 = inp["f_w13"][l]
        cols = [_chunk_cols(2 * i * 128, FFH + 2 * i * 128, (2 * i + 1) * 128, FFH + (2 * i + 1) * 128)
                for i in range(NHC // 2)]
        sh["f%d_up" % l] = _units(w13, cols)
        w2 = inp["f_w2"][l]
        dn = np.zeros((6, 128, NK, 512), np.float32)
        for grp in range(3):
            nh = min(8, NHC - grp * 8)
            for half in range(2):
                blk = w2[grp * 1024: grp * 1024 + nh * 128, half * 512:(half + 1) * 512]
                dn[grp * 2 + half, :, :nh, :] = blk.reshape(nh, 128, 512).transpose(1, 0, 2)
        sh["f%d_dn" % l] = dn.reshape(6, 128, NK * 512)
    for ia in range(inp["a_w_in"].shape[0]):
        cols = [_chunk_cols(m * 128, D + m * 128, 2 * D + m * 128) for m in range(NK)]
        sh["a%d_in" % ia] = _units(inp["a_w_in"][ia], cols)
        sh["a%d_out" % ia] = _units(inp["a_w_out"][ia], [np.arange(0, 512), np.arange(512, 1024)])
        cw = inp["a_conv_w"][ia]
        sh["a%d_cw" % ia] = np.concatenate([_pvec(cw[j]) for j in range(3)], axis=1)
    for ib in range(inp["b_w_qkv"].shape[0]):
        cols = [_chunk_cols(m * 128, D + m * 128, 2 * D + m * 128) for m in range(NK)]
        sh["b%d_in" % ib] = _units(inp["b_w_qkv"][ib], cols)
        sh["b%d_out" % ib] = _units(inp["b_w_out"][ib], [np.arange(0, 512), np.arange(512, 1024)])
        rpb = inp["b_rpb"][ib]
        kr = np.arange(2)[:, None, None, None]
        kc = np.arange(64)[None, :, None, None]
        qr = np.arange(2)[None, None, :, None]
        qc = np.arange(64)[None, None, None, :]
        dc = np.clip(kc - qc + 15, 0, 30) + 0 * kr + 0 * qr
        tabs = np.zeros((16, 7, 128, 128), np.float32)
        for oi in range(7):
            dr = 2 * (oi - 3) + kr - qr + 7 + 0 * kc + 0 * qc
            ok = (dr >= 0) & (dr <= 14)
            g = rpb[:, np.clip(dr, 0, 14), dc]
            g = np.where(ok[None], g, np.float32(0.0))
            tabs[:, oi] = g.reshape(16, 128, 128)
        sh["b%d_rpbT" % ib] = np.ascontiguousarray(
            tabs.reshape(8, 2, 7, 128, 128).transpose(0, 3, 1, 2, 4)).reshape(8, 128, 2 * 7 * 128)
        qkg = np.zeros((128, 2), np.float32)
        qkg[:, 0] = np.tile(inp["b_q_norm_g"][ib], 2)
        qkg[:, 1] = np.tile(inp["b_k_norm_g"][ib], 2)
        sh["b%d_qkg" % ib] = qkg
    for ic in range(inp["c_w_in"].shape[0]):
        cols = [_chunk_cols(c * 128, D + c * 128, 2 * D + c * 128) for c in range(NK)]
        sh["c%d_in" % ic] = _units(inp["c_w_in"][ic], cols)
        sh["c%d_out" % ic] = _units(inp["c_w_out"][ic], [np.arange(0, 512), np.arange(512, 1024)])
        sw = inp["c_short_w"][ic]
        sb = inp["c_short_b"][ic]
        cs = np.zeros((128, 96), np.float32)
        for tap in range(3):
            cs[:, tap * 24:(tap + 1) * 24] = _pvec(sw[tap])
        cs[:, 72:96] = _pvec(sb)
        sh["c%d_cs" % ic] = cs
        sh["c%d_w1" % ic] = np.ascontiguousarray(inp["c_f_w1"][ic]).astype(np.float32)
        sh["c%d_w2" % ic] = np.ascontiguousarray(inp["c_f_w2"][ic]).astype(np.float32)
        sh["c%d_w3" % ic] = np.ascontiguousarray(inp["c_f_w3"][ic]).astype(np.float32)
        fv = np.zeros((64, 4), np.float32)
        fv[:, 0] = inp["c_f_b1"][ic]
        fv[:, 1] = inp["c_f_b2"][ic]
        fv[:, 2] = inp["c_f_freq"][ic]
        sh["c%d_fv" % ic] = fv
        sh["c%d_skip" % ic] = np.ascontiguousarray(inp["c_f_skip"][ic]).astype(np.float32)
    return sh


class Builder:
    def __init__(self, shared_shapes, layers, nseq):
        self.layers = layers
        self.nseq = nseq
        nc = bass.Bass("TRN2", target_bir_lowering=False)
        self.nc = nc
        self.P = Prog()
        self.dram = {}
        for name, (shape, dt) in shared_shapes.items():
            self.dram[name] = nc.dram_tensor(name, list(shape), dt, kind="ExternalInput").ap()
        self.x_in = nc.dram_tensor("x", [nseq, 128, NK, L], F32, kind="ExternalInput").ap()
        self.y_out = nc.dram_tensor("y", [nseq, 128, NK, L], F32, kind="ExternalOutput").ap()
        self.bank_rr = 0
        self.slot_rr = 0
        self.uid = 0

    def alloc(self):
        nc = self.nc
        self.xT = nc.alloc_sbuf_tensor("sb_xT", [128, NK, L], F32)
        self.hT = nc.alloc_sbuf_tensor("sb_hT", [128, NK, L], BF16)
        self.gT = nc.alloc_sbuf_tensor("sb_gT", [128, NK, L], BF16)
        self.NSLOT = 3
        self.ws = [nc.alloc_sbuf_tensor("sb_ws%d" % i, [128, 4096], BF16) for i in range(self.NSLOT)]
        self.SCRB = 51328
        self.scr = nc.alloc_sbuf_tensor("sb_scr", [128, self.SCRB // 4], F32)
        self.gvec = nc.alloc_sbuf_tensor("sb_gvec", [128, DEPTH * 2 * NK], F32)
        self.onesmean = nc.alloc_sbuf_tensor("sb_onesmean", [128, 128], BF16)
        self.small = nc.alloc_sbuf_tensor("sb_small", [128, 64], F32)
        self.ident = nc.alloc_sbuf_tensor("sb_ident", [128, 128], BF16)
        self.headones = nc.alloc_sbuf_tensor("sb_headones", [128, 128], BF16)
        self.ones128 = nc.alloc_sbuf_tensor("sb_ones128", [128, 128], BF16)
        self.hsk = nc.alloc_sbuf_tensor("sb_hsk", [128, 16], F32)
        self.Hs = nc.dram_tensor("hs_scratch", [2, 2, 16, 128, 1024], BF16, kind="ExternalOutput").ap()
        self.ps = [nc.alloc_psum_tensor("ps%d" % i, [128, 512], F32) for i in range(8)]

    def scr_view(self, off, shape, dt):
        esz = 4 if dt == F32 else 2
        n = int(np.prod(shape[1:]))
        assert off % 4 == 0 and off + n * esz <= self.SCRB, (off, n, esz)
        ap = self.scr[:, off // 4: (off + n * esz) // 4]
        if dt != F32:
            ap = ap.bitcast(dt)
        if len(shape) == 3:
            ap = ap.rearrange("p (a b) -> p a b", a=shape[1])
        return ap

    def bank(self):
        b = self.bank_rr
        self.bank_rr = (b + 1) % 8
        return b

    def wload(self, name, u, ncols_total):
        s = self.slot_rr
        self.slot_rr = (s + 1) % self.NSLOT
        self.P.dma("pool", self.ws[s][:, 0:ncols_total], self.dram[name][u], writes=[("ws", s)])
        return s

    def mm(self, b, lhsT, rhs, start, stop, reads):
        ps = self.ps[b]
        n = rhs.shape[-1]
        m = lhsT.shape[-1]
        self.P.op("pe", lambda e: e.matmul(ps[0:m, 0:n], lhsT=lhsT, rhs=rhs, start=start, stop=stop, skip_group_check=True),
                  reads=reads, writes=[("ps", b)])

    def tkeys(self, name, k, tts=range(NTT)):
        return [(name, k, t) for t in tts]

    def rmsnorm(self, gcol):
        P = self.P
        sq = [self.scr_view(i * 1024, [128, TT], BF16) for i in range(2)]
        rt = self.scr_view(2048, [128, TT], F32)
        rstd = self.scr_view(4096, [128, TT], F32)
        for tt in range(NTT):
            sl = slice(tt * TT, (tt + 1) * TT)
            b = self.bank()
            for k in range(NK):
                q = sq[k % 2]
                xin = self.xT[:, k, sl]
                P.op("act", lambda e, q=q, xin=xin: e.activation(out=q, in_=xin, func=AF.Square),
                     reads=[("xT", k, tt)], writes=[("sq", k % 2)])
                self.mm(b, self.onesmean[:], q, k == 0, k == NK - 1, [("sq", k % 2), "onesmean"])
            psb = self.ps[b]
            P.op("act", lambda e, psb=psb: e.activation(out=rt, in_=psb[:], func=AF.Ln, bias=EPS, scale=1.0),
                 reads=[("ps", b)], writes=["rt"])
            P.op("act", lambda e: e.activation(out=rstd, in_=rt, func=AF.Exp, scale=-0.5), reads=["rt"], writes=["rstd"])
            for k in range(NK):
                xin = self.xT[:, k, sl]
                hout = self.hT[:, k, sl]
                gap = self.gvec[:, gcol + k: gcol + k + 1]
                P.op("dve", lambda e, xin=xin, hout=hout, gap=gap: e.scalar_tensor_tensor(
                    out=hout, in0=xin, scalar=gap, in1=rstd, op0=ALU.mult, op1=ALU.mult),
                    reads=[("xT", k, tt), "rstd", "gvec"], writes=[("hT", k, tt)])

    def proj_accum_x(self, wname, src, src_name, nkc_list):
        P = self.P
        for half in range(2):
            s = self.wload(wname, half, NK * 512) if not isinstance(wname, tuple) else self.wload(wname[0], wname[1] + half, NK * 512)
            w = self.ws[s][:, :].rearrange("p (k c) -> p k c", k=NK)
            for fo in range(4):
                kx = half * 4 + fo
                for tt in range(NTT):
                    sl = slice(tt * TT, (tt + 1) * TT)
                    b = self.bank()
                    for i, kc in enumerate(nkc_list):
                        self.mm(b, w[:, kc, fo * 128:(fo + 1) * 128], src[:, kc, sl], i == 0, i == len(nkc_list) - 1,
                                [("ws", s), (src_name, kc, tt)])
                    psb = self.ps[b]
                    xap = self.xT[:, kx, sl]
                    P.op("dve", lambda e, psb=psb, xap=xap: e.tensor_tensor(out=xap, in0=xap, in1=psb[:], op=ALU.add),
                         reads=[("ps", b), ("xT", kx, tt)], writes=[("xT", kx, tt)])

    def ffn(self, l):
        P = self.P
        self.rmsnorm((l * 2 + 1) * NK)
        aT = self.gT
        sg = [self.scr_view(8192 + i * 2048, [128, TT], F32) for i in range(2)]
        sgi = 0
        for grp in range(3):
            nh = min(8, NHC - grp * 8)
            for pair in range(nh // 2):
                u = grp * 4 + pair
                s = self.wload("f%d_up" % l, u, NK * 512)
                w = self.ws[s][:, :].rearrange("p (k c) -> p k c", k=NK)
                for hh in range(2):
                    hc = pair * 2 + hh
                    for tt in range(NTT):
                        sl = slice(tt * TT, (tt + 1) * TT)
                        bg, bu = self.bank(), self.bank()
                        for k in range(NK):
                            self.mm(bg, w[:, k, (2 * hh) * 128:(2 * hh + 1) * 128], self.hT[:, k, sl], k == 0, k == NK - 1,
                                    [("ws", s), ("hT", k, tt)])
                        for k in range(NK):
                            self.mm(bu, w[:, k, (2 * hh + 1) * 128:(2 * hh + 2) * 128], self.hT[:, k, sl], k == 0, k == NK - 1,
                                    [("ws", s), ("hT", k, tt)])
                        sgt = sg[sgi % 2]
                        sgk = ("sg", sgi % 2)
                        sgi += 1
                        psg, psu = self.ps[bg], self.ps[bu]
                        P.op("act", lambda e, sgt=sgt, psg=psg: e.activation(out=sgt, in_=psg[:], func=AF.Silu),
                             reads=[("ps", bg)], writes=[sgk])
                        aout = aT[:, hc, sl]
                        P.op("dve", lambda e, sgt=sgt, psu=psu, aout=aout: e.tensor_tensor(out=aout, in0=sgt, in1=psu[:], op=ALU.mult),
                             reads=[("ps", bu), sgk], writes=[("gT", hc, tt)])
            self.proj_accum_x(("f%d_dn" % l, grp * 2), aT, "gT", list(range(nh)))

    def mixer_a(self, l, ia):
        P = self.P
        self.rmsnorm((l * 2) * NK)
        cwt = self.small
        us = [self.scr_view(6144 + i * 2048, [128, TT], F32) for i in range(2)]
        cu = [self.scr_view(10240 + i * 8208, [128, L + 2], F32) for i in range(2)]
        Bs = [self.scr_view(26656 + i * 4096, [128, L], BF16) for i in range(2)]
        t1 = self.scr_view(34848, [128, L], F32)
        usi = 0
        for i in range(2):
            c = cu[i]
            P.op("dve", lambda e, c=c: e.memset(c[:, 0:1], 0.0), writes=[("cu", i, "h0")])
            P.op("dve", lambda e, c=c: e.memset(c[:, L + 1:L + 2], 0.0), writes=[("cu", i, "h1")])
        P.dma("sp", cwt[:, 0:24], self.dram["a%d_cw" % ia], writes=["cwt"])
        for m in range(NK):
            s = self.wload("a%d_in" % ia, m, NK * 384)
            w = self.ws[s][:, 0:NK * 384].rearrange("p (k c) -> p k c", k=NK)
            c = cu[m % 2]
            Bm = Bs[m % 2]
            for tt in range(NTT):
                sl = slice(tt * TT, (tt + 1) * TT)
                bb, bc, bu = self.bank(), self.bank(), self.bank()
                for j, b in enumerate((bb, bc, bu)):
                    for k in range(NK):
                        self.mm(b, w[:, k, j * 128:(j + 1) * 128], self.hT[:, k, sl], k == 0, k == NK - 1,
                                [("ws", s), ("hT", k, tt)])
                psb_, psc_, psu_ = self.ps[bb], self.ps[bc], self.ps[bu]
                bout = Bm[:, sl]
                P.op("act", lambda e, bout=bout, psb_=psb_: e.activation(out=bout, in_=psb_[:], func=AF.Copy),
                     reads=[("ps", bb)], writes=[("Bs", m % 2, tt)])
                ut = us[usi % 2]
                uk = ("us", usi % 2)
                usi += 1
                P.op("act", lambda e, ut=ut, psu_=psu_: e.activation(out=ut, in_=psu_[:], func=AF.Copy),
                     reads=[("ps", bu)], writes=[uk])
                cout = c[:, 1 + tt * TT: 1 + (tt + 1) * TT]
                P.op("dve", lambda e, cout=cout, psc_=psc_, ut=ut: e.tensor_tensor(out=cout, in0=ut, in1=psc_[:], op=ALU.mult),
                     reads=[("ps", bc), uk], writes=[("cu", m % 2, tt)])
            cuk = [("cu", m % 2, tt) for tt in range(NTT)] + [("cu", m % 2, "h0"), ("cu", m % 2, "h1")]
            w0 = cwt[:, 0 * 8 + m: 0 * 8 + m + 1]
            w1 = cwt[:, 1 * 8 + m: 1 * 8 + m + 1]
            w2 = cwt[:, 2 * 8 + m: 2 * 8 + m + 1]
            P.op("dve", lambda e, c=c, w1=w1: e.tensor_scalar(out=t1, in0=c[:, 1:L + 1], scalar1=w1, scalar2=None, op0=ALU.mult),
                 reads=cuk + ["cwt"], writes=["t1"])
            P.op("dve", lambda e, c=c, w0=w0: e.scalar_tensor_tensor(out=t1, in0=c[:, 0:L], scalar=w0, in1=t1, op0=ALU.mult, op1=ALU.add),
                 reads=cuk + ["cwt", "t1"], writes=["t1"])
            P.op("dve", lambda e, c=c, w2=w2: e.scalar_tensor_tensor(out=t1, in0=c[:, 2:L + 2], scalar=w2, in1=t1, op0=ALU.mult, op1=ALU.add),
                 reads=cuk + ["cwt", "t1"], writes=["t1"])
            gout = self.gT[:, m, :]
            P.op("dve", lambda e, gout=gout, Bm=Bm: e.tensor_tensor(out=gout, in0=t1, in1=Bm, op=ALU.mult),
                 reads=["t1"] + [("Bs", m % 2, tt) for tt in range(NTT)], writes=self.tkeys("gT", m))
        self.proj_accum_x("a%d_out" % ia, self.gT, "gT", list(range(NK)))


    @staticmethod
    def na_keys(j):
        if j <= 1:
            return list(range(0, 4))
        if j >= 14:
            return list(range(12, 16))
        return list(range(j - 2, j + 3))

    def mixer_b(self, l, ib):
        P = self.P
        self.rmsnorm((l * 2) * NK)
        sq = [self.scr_view(i * 1024, [128, TT], BF16) for i in range(2)]
        rt = self.scr_view(2048, [128, TT], F32)
        rstd = self.scr_view(4096, [128, TT], F32)
        QK = [self.scr_view(6144, [128, L], BF16), self.scr_view(10240, [128, L], BF16)]
        Vx = self.scr_view(14336, [128, 32, 128], BF16)
        stg = self.scr_view(22528, [128, 14, 128], F32)
        tab = self.scr_view(29696, [128, 18, 128], BF16)
        PT = [self.scr_view(34304 + i * 1536, [128, 768], BF16) for i in range(2)]
        R = [self.scr_view(37376 + i * 2048, [128, TT], F32) for i in range(2)]
        msk = self.scr_view(41472, [128, 3, 128], F32)
        rtB = self.scr_view(47232, [128, TT], F32)
        rstdB = self.scr_view(49280, [128, TT], F32)
        qkg = self.small[:, 32:34]
        P.dma("sp", msk, self.dram["namask"].rearrange("p (a b) -> p a b", a=3), writes=["msk"])
        P.dma("sp", qkg, self.dram["b%d_qkg" % ib], writes=["qkg"])
        Vx4 = Vx.rearrange("p (c h) f -> p c h f", h=2)
        P.op("dve", lambda e: e.memset(Vx4[:, :, 0, 64:128], 1.0), writes=["vx_ones_a"])
        P.op("dve", lambda e: e.memset(Vx4[:, :, 1, 0:64], 1.0), writes=["vx_ones_b"])
        J = {c: [j for j in range(16) if c in self.na_keys(j)] for c in range(16)}
        pti = 0
        ri = 0
        for m in range(NK):
            s = self.wload("b%d_in" % ib, m, NK * 384)
            w = self.ws[s][:, 0:NK * 384].rearrange("p (k c) -> p k c", k=NK)
            P.dma("sp", stg, self.dram["b%d_rpbT" % ib][m].rearrange("p (a b) -> p a b", a=14), writes=["stg"])
            for j in range(2):
                for tt in range(NTT):
                    sl = slice(tt * TT, (tt + 1) * TT)
                    b = self.bank()
                    for k in range(NK):
                        self.mm(b, w[:, k, j * 128:(j + 1) * 128], self.hT[:, k, sl], k == 0, k == NK - 1,
                                [("ws", s), ("hT", k, tt)])
                    psb = self.ps[b]
                    q = sq[(j * NTT + tt) % 2]
                    qk_ = ("sq", (j * NTT + tt) % 2)
                    P.op("act", lambda e, q=q, psb=psb: e.activation(out=q, in_=psb[:], func=AF.Square),
                         reads=[("ps", b)], writes=[qk_])
                    b2 = self.bank()
                    self.mm(b2, self.headones[:], q, True, True, [qk_, "headones"])
                    ps2 = self.ps[b2]
                    par = (j * NTT + tt) % 2
                    rt_, rstd_ = (rt, rstd) if par == 0 else (rtB, rstdB)
                    rtk, rsk = ("rt" if par == 0 else "rtB"), ("rstd" if par == 0 else "rstdB")
                    P.op("act", lambda e, ps2=ps2, rt_=rt_: e.activation(out=rt_, in_=ps2[:], func=AF.Ln, bias=EPS, scale=1.0),
                         reads=[("ps", b2)], writes=[rtk])
                    P.op("act", lambda e, rt_=rt_, rstd_=rstd_: e.activation(out=rstd_, in_=rt_, func=AF.Exp, scale=-0.5), reads=[rtk], writes=[rsk])
                    qout = QK[j][:, sl]
                    gap = qkg[:, j:j + 1]
                    P.op("dve", lambda e, qout=qout, psb=psb, gap=gap, rstd_=rstd_: e.scalar_tensor_tensor(
                        out=qout, in0=psb[:], scalar=gap, in1=rstd_, op0=ALU.mult, op1=ALU.mult),
                        reads=[("ps", b), rsk, "qkg"], writes=[("QK", j, tt)])
            for g4 in range(4):
                b = self.bank()
                for i in range(4):
                    kc = g4 * 4 + i
                    for k in range(NK):
                        psv = self.ps[b]
                        lhsT = self.hT[:, k, kc * 128:(kc + 1) * 128]
                        rhs = w[:, k, 256:384]
                        P.op("pe", lambda e, psv=psv, lhsT=lhsT, rhs=rhs, i=i, k=k: e.matmul(
                            psv[:, i * 128:(i + 1) * 128], lhsT=lhsT, rhs=rhs, start=(k == 0), stop=(k == NK - 1),
                            skip_group_check=True),
                            reads=[("ws", s), ("hT", k, kc // 4)], writes=[("ps", b)])
                psv = self.ps[b]
                pv3 = psv[:, :].rearrange("p (i f) -> p i f", i=4)
                oa = Vx4[:, g4 * 4:(g4 + 1) * 4, 0, 0:64]
                ob = Vx4[:, g4 * 4:(g4 + 1) * 4, 1, 64:128]
                P.op("act", lambda e, oa=oa, pv3=pv3: e.activation(out=oa, in_=pv3[:, :, 0:64], func=AF.Copy),
                     reads=[("ps", b)], writes=[("Vx", g4, 0)])
                P.op("act", lambda e, ob=ob, pv3=pv3: e.activation(out=ob, in_=pv3[:, :, 64:128], func=AF.Copy),
                     reads=[("ps", b)], writes=[("Vx", g4, 1)])
            blk_src = [(5, 1), (4, 0), (3, 0), (2, 0), (1, 2), (6, 0), (5, 0), (1, 0), (0, 0)]
            for h in range(2):
                for bi, (oi, mv) in enumerate(blk_src):
                    tout = tab[:, h * 9 + bi, :]
                    tin = stg[:, h * 7 + oi, :]
                    mk = msk[:, mv, :]
                    P.op("dve", lambda e, tout=tout, tin=tin, mk=mk: e.scalar_tensor_tensor(
                        out=tout, in0=tin, scalar=8.0, in1=mk, op0=ALU.mult, op1=ALU.add),
                        reads=["stg", "msk"], writes=[("tab", h)])
            for h in range(2):
                pb = 64 * h
                od = [0, 1, 2, 3]
                for g in od:
                    psg = self.ps[g]
                    P.op("dve", lambda e, psg=psg: e.memset(psg[:], 0.0), writes=[("ps", g)])
                pend = []

                def emit_pv(c, js, pt, ptk, h=h):
                    vx = Vx[:, c * 2 + h, :]
                    ji = 0
                    while ji < len(js):
                        g = js[ji] // 4
                        je = ji
                        while je + 1 < len(js) and js[je + 1] // 4 == g:
                            je += 1
                        cnt = je - ji + 1
                        psg = self.ps[g]
                        o0 = (js[ji] % 4) * 128
                        rhs = pt[:, ji * 128:(je + 1) * 128]
                        P.op("pe", lambda e, psg=psg, vx=vx, rhs=rhs, o0=o0, cnt=cnt: e.matmul(
                            psg[:, o0:o0 + cnt * 128], lhsT=vx, rhs=rhs, start=False, stop=False, skip_group_check=True),
                            reads=[ptk, ("Vx", c // 4, h), "vx_ones_a", "vx_ones_b"], writes=[("ps", g)])
                        ji = je + 1

                for c in range(16):
                    js = J[c]
                    jlo = js[0]
                    n = len(js) * 128
                    sb = [4, 5] if (pti % 2 == 0) else [6, 7]
                    pt = PT[pti % 2]
                    ptk = ("PT", pti % 2)
                    pti += 1
                    kT = QK[1][pb:pb + 64, c * 128:(c + 1) * 128]
                    segs = [(0, min(n, 512))] + ([(512, n)] if n > 512 else [])
                    for si, (a0, a1) in enumerate(segs):
                        pss = self.ps[sb[si]]
                        qT = QK[0][pb:pb + 64, jlo * 128 + a0: jlo * 128 + a1]
                        P.op("pe", lambda e, pss=pss, kT=kT, qT=qT, a0=a0, a1=a1: e.matmul(
                            pss[:, 0:a1 - a0], lhsT=kT, rhs=qT, start=True, stop=False, skip_group_check=True),
                            reads=[("QK", 0, t) for t in range(NTT)] + [("QK", 1, c // 4)], writes=[("ps", sb[si])])
                    interior = 4 <= c <= 11
                    if interior:
                        for si, (a0, a1) in enumerate(segs):
                            pss = self.ps[sb[si]]
                            tb = tab[:, h * 9: h * 9 + 5, :].rearrange("p a b -> p (a b)")[:, a0:a1]
                            P.op("pe", lambda e, pss=pss, tb=tb, a0=a0, a1=a1: e.matmul(
                                pss[:, 0:a1 - a0], lhsT=self.ident[:], rhs=tb, start=False, stop=True, skip_group_check=True),
                                reads=[("tab", h), "ident"], writes=[("ps", sb[si])])
                    else:
                        for ji, j in enumerate(js):
                            o = c - j
                            masked = (2 <= j <= 13) and abs(o) == 2
                            bi = {2: (0 if masked else 6), 1: 1, 0: 2, -1: 3, -2: (4 if masked else 7), 3: 5, -3: 8}[o]
                            col = ji * 128
                            si = col // 512
                            pss = self.ps[sb[si]]
                            tb = tab[:, h * 9 + bi, :]
                            cc = col - si * 512
                            P.op("pe", lambda e, pss=pss, tb=tb, cc=cc: e.matmul(
                                pss[:, cc:cc + 128], lhsT=self.ident[:], rhs=tb, start=False, stop=True, skip_group_check=True),
                                reads=[("tab", h), "ident"], writes=[("ps", sb[si])])
                    for si, (a0, a1) in enumerate(segs):
                        pss = self.ps[sb[si]]
                        pto = pt[:, a0:a1]
                        P.op("act", lambda e, pss=pss, pto=pto, a0=a0, a1=a1: e.activation(
                            out=pto, in_=pss[:, 0:a1 - a0], func=AF.Exp, scale=0.125),
                            reads=[("ps", sb[si])], writes=[ptk])
                    pend.append((c, js, pt, ptk))
                    if len(pend) > 1:
                        emit_pv(*pend.pop(0))
                while pend:
                    emit_pv(*pend.pop(0))
                for g in od:
                    psg = self.ps[g]
                    r = R[ri % 2]
                    rk = ("R", ri % 2)
                    ri += 1
                    dpb = 64 - pb
                    P.op("act", lambda e, psg=psg, r=r, dpb=dpb: e.activation(out=r[dpb:dpb + 64, :], in_=psg[dpb:dpb + 64, :], func=AF.Ln),
                         reads=[("ps", g)], writes=[rk])
                    P.op("act", lambda e, r=r, dpb=dpb: e.activation(out=r[dpb:dpb + 64, :], in_=r[dpb:dpb + 64, :], func=AF.Exp, scale=-1.0),
                         reads=[rk], writes=[rk])
                    gout = self.gT[pb:pb + 64, m, g * 512:(g + 1) * 512]
                    P.op("dve", lambda e, psg=psg, r=r, dpb=dpb, gout=gout, pb=pb: e.tensor_tensor(
                        out=gout, in0=psg[pb:pb + 64, :], in1=r[dpb:dpb + 64, :], op=ALU.mult),
                        reads=[("ps", g), rk], writes=[("gT", m, g)])
        self.proj_accum_x("b%d_out" % ib, self.gT, "gT", list(range(NK)))

    def sload(self, dram_ap, ncols=4096):
        s = self.slot_rr
        self.slot_rr = (s + 1) % self.NSLOT
        self.P.dma(os.environ.get("HY_SQ", "pool"), self.ws[s][:, 0:ncols], dram_ap, writes=[("ws", s)])
        return s

    def sin_rr(self, out, in_, tmp, tmp2, np_, n, rkeys, wkeys):
        P = self.P
        MAGIC = 12582912.0
        TWO_PI = 2.0 * math.pi
        P.op("dve", lambda e: e.tensor_scalar(out=tmp, in0=in_, scalar1=1.0 / TWO_PI, scalar2=MAGIC, op0=ALU.mult, op1=ALU.add),
             reads=rkeys, writes=["srr_t"])
        P.op("dve", lambda e: e.tensor_scalar(out=tmp2, in0=tmp, scalar1=MAGIC, scalar2=-TWO_PI, op0=ALU.subtract, op1=ALU.mult),
             reads=["srr_t"], writes=["srr_t2"])
        P.op("dve", lambda e: e.tensor_tensor(out=tmp, in0=in_, in1=tmp2, op=ALU.add),
             reads=rkeys + ["srr_t2", "srr_t"], writes=["srr_t"])
        P.op("act", lambda e: e.activation(out=out, in_=tmp, func=AF.Sin, scale=1.0 - 2e-6),
             reads=["srr_t"], writes=wkeys)

    def hyena_filter(self, ic):
        P = self.P
        INV = 2.0 / 4096.0
        hreg = self.hT[:, :, :].rearrange("p a b -> p (a b)").bitcast(F32)
        zT = hreg[0:33, 0:2048]
        h1 = hreg[0:64, 2048:4096]
        hTb = self.hT[:, :, :].rearrange("p a b -> p (a b)")
        h2 = hTb[0:64, 8192:10240]
        w3f = hTb[0:64, 12288:12800]
        w3r = hTb[0:64, 12800:13312]
        w1 = hreg[0:33, 7168:7232]
        w2 = hreg[0:64, 7232:7296]
        fv = hreg[0:64, 7296:7300]
        fb = hreg[0:64, 7300:7302]
        pre = hreg[0:64, 7424:7936]
        tA = hreg[0:64, 7936:8192]
        greg = self.gT[:, :, :].rearrange("p a b -> p (a b)")
        s_tok = greg[:, 0:8192].rearrange("p (a b) -> p a b", a=16)
        d_tok = greg[:, 8192:16384].rearrange("p (a b) -> p a b", a=16)
        sv = lambda off, shape, dt: self.scr_view(off, shape, dt)
        tmp = sv(0, [128, TT], F32)
        tmp2 = sv(2048, [128, TT], F32)
        dec = sv(4096, [128, TT], F32)
        ff = sv(6144, [128, TT], F32)
        rr = sv(8192, [128, TT], F32)
        ab = [sv(10240 + i * 1024, [128, TT], BF16) for i in range(2)]
        a1 = sv(12288, [128, TT], F32)
        rn2 = sv(14336, [128, TT], F32)
        nrn2 = sv(16384, [128, TT], F32)
        skip2 = sv(18432, [128, TT], F32)
        drow = sv(20480, [128, TT], F32)
        hb = [sv(22528 + i * 2048, [128, 2, TT], BF16) for i in range(2)]
        negt = sv(26624, [128, 16], F32)
        a3 = sv(28672, [128, TT], F32)
        P.dma("sp", zT, self.dram["hy_zT"], writes=["zT"])
        P.dma("sp", w1, self.dram["c%d_w1" % ic], writes=["fw1"])
        P.dma("sp", w2, self.dram["c%d_w2" % ic], writes=["fw2"])
        P.dma("sp", fv, self.dram["c%d_fv" % ic], writes=["fv"])
        P.dma("sp", negt, self.dram["hy_negt"], writes=["negt"])
        P.op("dve", lambda e: e.tensor_tensor(out=fb[:, 0:1], in0=fv[:, 0:1], in1=fv[:, 2:3], op=ALU.mult), reads=["fv"], writes=["fb0"])
        P.op("dve", lambda e: e.tensor_tensor(out=fb[:, 1:2], in0=fv[:, 1:2], in1=fv[:, 2:3], op=ALU.mult), reads=["fv"], writes=["fb1"])
        for li, (wl, src, dst, np_in) in enumerate(((w1, zT, h1, 33), (w2, h1, h2, 64))):
            for tt in range(NTT):
                sl = slice(tt * TT, (tt + 1) * TT)
                b = self.bank()
                psb = self.ps[b]
                rhs = src[:, sl]
                P.op("pe", lambda e, psb=psb, wl=wl, rhs=rhs: e.matmul(psb[0:64, :], lhsT=wl, rhs=rhs, start=True, stop=True, skip_group_check=True),
                     reads=["fw1", "fw2", "zT", ("h1", tt)], writes=[("ps", b)])
                fbi = fb[:, li:li + 1]
                P.op("act", lambda e, psb=psb, fbi=fbi: e.activation(out=pre, in_=psb[0:64, :], func=AF.Identity, bias=fbi, scale=fv[:, 2:3]),
                     reads=[("ps", b), "fv", "fb0", "fb1"], writes=["fpre"])
                self.sin_rr(dst[:, sl], pre, tmp[0:64, :], tmp2[0:64, :], 64, TT, ["fpre"], [("h%d" % (li + 1), tt)])
        import os
        if os.environ.get("HY_STAGE") == "1":
            return
        h2keys = [("h2", tt) for tt in range(NTT)]
        abi = 0
        hbi = 0
        for o in range(2):
            for cg in range(2):
                colf = o * 2048 + cg * 512
                colr = colf + 1024
                if not (os.environ.get("HY_NOW3") and (o, cg) != (0, 0)):
                    P.dma("pool", w3f, self.dram["c%d_w3" % ic][:, colf:colf + 512], writes=["w3f"])
                    P.dma("pool", w3r, self.dram["c%d_w3" % ic][:, colr:colr + 512], writes=["w3r"])
                P.dma("sp", drow, self.dram["hy_delta"][cg], writes=["drow"])
                skrow = self.dram["c%d_skip" % ic][o:o + 1, cg * 512:(cg + 1) * 512].partition_broadcast(128)
                P.dma("sp", skip2, skrow, writes=["skip2"])
                P.op("dve", lambda e: e.tensor_scalar(out=skip2, in0=skip2, scalar1=INV, scalar2=None, op0=ALU.mult),
                     reads=["skip2"], writes=["skip2"])
                if os.environ.get("HY_ONEIT") and (o, cg) != (0, 0):
                    return
                cut2 = os.environ.get("HY_CUT2", "") if (o, cg) != (0, 0) else ""
                if os.environ.get("HY_CUT") == "a" or cut2 == "a":
                    return
                bn = self.bank()
                psn = self.ps[bn]
                ntc = int(os.environ.get("HY_TC2", "16")) if (o, cg) != (0, 0) else 16
                for tc in range(ntc):
                    bf_, br_ = self.bank(), self.bank()
                    if bf_ == bn or br_ == bn:
                        bf_, br_ = self.bank(), self.bank()
                    psf, psr = self.ps[bf_], self.ps[br_]
                    lh = h2[:, tc * 128:(tc + 1) * 128]
                    P.op("pe", lambda e, psf=psf, lh=lh: e.matmul(psf[:], lhsT=lh, rhs=w3f, start=True, stop=True, skip_group_check=True),
                         reads=h2keys + ["w3f"], writes=[("ps", bf_)])
                    P.op("pe", lambda e, psr=psr, lh=lh: e.matmul(psr[:], lhsT=lh, rhs=w3r, start=True, stop=True, skip_group_check=True),
                         reads=h2keys + ["w3r"], writes=[("ps", br_)])
                    tcb = os.environ.get("HY_TCB", "") if (o, cg) != (0, 0) else ""
                    if tcb == "m":
                        continue
                    nt = negt[:, tc:tc + 1]
                    P.op("act", lambda e, nt=nt: e.activation(out=dec, in_=drow, func=AF.Exp, scale=nt),
                         reads=["drow", "negt"], writes=["dec"])
                    P.op("dve", lambda e, psf=psf: e.tensor_tensor(out=ff, in0=psf[:], in1=dec, op=ALU.mult),
                         reads=[("ps", bf_), "dec"], writes=["ff"])
                    P.op("dve", lambda e, psr=psr: e.tensor_tensor(out=rr, in0=psr[:], in1=dec, op=ALU.mult),
                         reads=[("ps", br_), "dec"], writes=["rr"])
                    if tcb == "e":
                        continue
                    so = s_tok[:, tc, :]
                    do = d_tok[:, tc, :]
                    P.op("dve", lambda e, so=so: e.tensor_tensor(out=so, in0=ff, in1=rr, op=ALU.add),
                         reads=["ff", "rr"], writes=[("s_tok", tc)])
                    P.op("dve", lambda e, do=do: e.tensor_tensor(out=do, in0=ff, in1=rr, op=ALU.subtract),
                         reads=["ff", "rr"], writes=[("d_tok", tc)])
                    if tcb == "s":
                        continue
                    a2 = ab[abi % 2]
                    ak = ("ab", abi % 2)
                    abi += 1
                    P.op("act", lambda e: e.activation(out=a1, in_=ff, func=AF.Abs), reads=["ff"], writes=["a1"])
                    P.op("act", lambda e: e.activation(out=a3, in_=rr, func=AF.Abs), reads=["rr"], writes=["a3"])
                    P.op("dve", lambda e, a2=a2: e.tensor_tensor(out=a2, in0=a1, in1=a3, op=ALU.add),
                         reads=["a3", "a1"], writes=[ak])
                    if tc == 0:
                        P.op("dve", lambda e: e.tensor_tensor(out=a1[0:1, :], in0=ff[0:1, :], in1=rr[0:1, :], op=ALU.add),
                             reads=["ff", "rr", ak], writes=["a1"])
                        P.op("dve", lambda e, a2=a2: e.scalar_tensor_tensor(out=a2[0:1, :], in0=a1[0:1, :], scalar=-1.0, in1=a1[0:1, :], op0=ALU.mult, op1=ALU.max),
                             reads=["a1"], writes=[ak])
                    P.op("pe", lambda e, a2=a2, tc=tc, psn=psn: e.matmul(psn[:], lhsT=self.ones128[:], rhs=a2, start=(tc == 0), stop=(tc == 15), skip_group_check=True),
                         reads=[ak, "ones128"], writes=[("ps", bn)])
                if os.environ.get("HY_CUT") == "b" or cut2 == "b" or ntc < 16:
                    return
                P.op("dve", lambda e, psn=psn: e.reciprocal(out=rn2, in_=psn[:]), reads=[("ps", bn)], writes=["rn2"])
                P.op("dve", lambda e: e.tensor_scalar(out=rn2, in0=rn2, scalar1=INV, scalar2=None, op0=ALU.mult), reads=["rn2"], writes=["rn2"])
                P.op("dve", lambda e: e.tensor_scalar(out=nrn2, in0=rn2, scalar1=-1.0, scalar2=None, op0=ALU.mult), reads=["rn2"], writes=["nrn2"])
                if os.environ.get("HY_CUT") == "d" or cut2 == "d":
                    return
                for i in range(int(os.environ.get("HY_NP", "16"))):
                    s = self.sload(self.dram["dft_f"][i])
                    Fu = self.ws[s][:, :].rearrange("p (a t r) -> p a t r", a=2, t=16)
                    if os.environ.get("HY_CUT") == "e":
                        continue
                    bA, bB = self.bank(), self.bank()
                    for tc in range(16):
                        self.mm(bA, Fu[:, 0, tc, :], s_tok[:, tc, :], tc == 0, tc == 15, [("ws", s), ("s_tok", tc)])
                    for tc in range(16):
                        self.mm(bB, Fu[:, 1, tc, :], d_tok[:, tc, :], tc == 0, tc == 15, [("ws", s), ("d_tok", tc)])
                    psA, psB = self.ps[bA], self.ps[bB]
                    hbt = hb[hbi % 2]
                    hk = ("hb", hbi % 2)
                    hbi += 1
                    if os.environ.get("HY_CUT") == "f":
                        continue
                    P.op("dve", lambda e, psA=psA: e.tensor_tensor(out=tmp, in0=psA[:], in1=rn2, op=ALU.mult),
                         reads=[("ps", bA), "rn2"], writes=["srr_t"])
                    P.op("dve", lambda e, hbt=hbt: e.tensor_tensor(out=hbt[:, 0, :], in0=tmp, in1=skip2, op=ALU.add),
                         reads=["srr_t", "skip2"], writes=[(hk, 0)])
                    P.op("dve", lambda e, hbt=hbt, psB=psB: e.tensor_tensor(out=hbt[:, 1, :], in0=psB[:], in1=nrn2, op=ALU.mult),
                         reads=[("ps", bB), "nrn2"], writes=[(hk, 1)])
                    if os.environ.get("HY_CUT") != "c":
                        P.dma("sp", self.Hs[o, cg, i], hbt.rearrange("p a b -> p (a b)"), reads=[(hk, 0), (hk, 1)], writes=[("Hs", o, cg, i)])
                pass

    def mixer_c(self, l, ic):
        P = self.P
        sv = self.scr_view
        t1 = sv(0, [128, 1024], F32)
        csw = sv(4096, [128, 96], F32)
        x1T = sv(6144, [128, 4, L], BF16)
        tok = sv(22528, [128, 16, TT], BF16)
        pre = sv(38912, [128, L + 2], F32)
        hb = [sv(38912 + i * 2048, [128, 2, TT], BF16) for i in range(2)]
        tm = [sv(43008 + i * 2048, [128, TT], F32) for i in range(2)]
        Z = self.hT[:, :, :].rearrange("p a b -> p (a b)").rearrange("p (a b) -> p a b", a=32)
        x2T = self.gT
        hbi = 0

        def transposes_to_tok(src, src_keyfn, c4):
            for g4 in range(4):
                b = self.bank()
                psT = self.ps[b][:, :].bitcast(BF16)
                for i in range(4):
                    tc = g4 * 4 + i
                    sin = src[:, tc * 128:(tc + 1) * 128]
                    P.op("pe", lambda e, psT=psT, sin=sin, i=i: e.transpose(out=psT[:, i * 128:(i + 1) * 128], in_=sin, identity=self.ident[:]),
                         reads=[src_keyfn(g4), "ident"], writes=[("ps", b)])
                tout = tok[:, g4 * 4:(g4 + 1) * 4, c4 * 128:(c4 + 1) * 128]
                P.op("act", lambda e, tout=tout, psT=psT: e.activation(out=tout, in_=psT[:, 0:512].rearrange("p (i f) -> p i f", i=4), func=AF.Copy),
                     reads=[("ps", b)], writes=[("tok", g4, c4)])

        def long_conv(o, cg, gate, gate_keyfn):
            nonlocal hbi
            for i in range(16):
                s = self.sload(self.dram["dft_f"][i])
                Fu = self.ws[s][:, :].rearrange("p (a t r) -> p a t r", a=2, t=16)
                hbt = hb[hbi % 2]
                hk = ("hbm", hbi % 2)
                hbi += 1
                P.dma("sp", hbt.rearrange("p a b -> p (a b)"), self.Hs[o, cg, i], reads=[("Hs", o, cg, i)], writes=[hk])
                bA, bB = self.bank(), self.bank()
                tokk = [("tok", g4, c4) for g4 in range(4) for c4 in range(4)]
                for tc in range(16):
                    self.mm(bA, Fu[:, 0, tc, :], tok[:, tc, :], tc == 0, tc == 15, [("ws", s)] + (tokk if tc == 0 else []))
                for tc in range(16):
                    self.mm(bB, Fu[:, 1, tc, :], tok[:, tc, :], tc == 0, tc == 15, [("ws", s)])
                psA, psB = self.ps[bA], self.ps[bB]
                Hr, Hi = hbt[:, 0, :], hbt[:, 1, :]
                z1, z2 = Z[:, i, :], Z[:, 16 + i, :]
                P.op("dve", lambda e, psA=psA, Hr=Hr: e.tensor_tensor(out=tm[0], in0=psA[:], in1=Hr, op=ALU.mult),
                     reads=[("ps", bA), hk], writes=[("tm", 0)])
                P.op("dve", lambda e, psB=psB, Hi=Hi: e.tensor_tensor(out=tm[1], in0=psB[:], in1=Hi, op=ALU.mult),
                     reads=[("ps", bB), hk], writes=[("tm", 1)])
                P.op("dve", lambda e, z1=z1: e.tensor_tensor(out=z1, in0=tm[0], in1=tm[1], op=ALU.add),
                     reads=[("tm", 0), ("tm", 1)], writes=[("Z", i)])
                P.op("dve", lambda e, psB=psB, Hr=Hr: e.tensor_tensor(out=tm[0], in0=psB[:], in1=Hr, op=ALU.mult),
                     reads=[("ps", bB), hk, ("tm", 0)], writes=[("tm", 0)])
                P.op("dve", lambda e, psA=psA, Hi=Hi: e.tensor_tensor(out=tm[1], in0=psA[:], in1=Hi, op=ALU.mult),
                     reads=[("ps", bA), hk, ("tm", 1)], writes=[("tm", 1)])
                P.op("dve", lambda e, z2=z2: e.tensor_tensor(out=z2, in0=tm[0], in1=tm[1], op=ALU.subtract),
                     reads=[("tm", 0), ("tm", 1)], writes=[("Z", 16 + i)])
            for tt in range(NTT):
                sl = slice(tt * TT, (tt + 1) * TT)
                acc = [self.bank() for _ in range(4)]
                for fg in range(4):
                    s = self.sload(self.dram["dft_i"][tt * 4 + fg])
                    G = self.ws[s][:, :].rearrange("p (a t) -> p a t", a=8)
                    for fci in range(8):
                        fc = fg * 8 + fci
                        for cc in range(4):
                            self.mm(acc[cc], Z[:, fc, cc * 128:(cc + 1) * 128], G[:, fci, :], fc == 0, fc == 31, [("ws", s), ("Z", fc)])
                for cc in range(4):
                    psy = self.ps[acc[cc]]
                    gap = gate[:, cc, sl]
                    P.op("dve", lambda e, psy=psy, gap=gap: e.tensor_tensor(out=gap, in0=gap, in1=psy[:], op=ALU.mult),
                         reads=[("ps", acc[cc]), gate_keyfn(cc, tt)], writes=[gate_keyfn(cc, tt)])

        for cg in range(2):
            P.barrier()
            self.rmsnorm((l * 2) * NK)
            P.barrier()
            P.dma("sp", csw, self.dram["c%d_cs" % ic], writes=["csw"])
            P.op("dve", lambda e: e.memset(pre[:, 0:1], 0.0), writes=[("pre", "h0")])
            P.op("dve", lambda e: e.memset(pre[:, L + 1:L + 2], 0.0), writes=[("pre", "h1")])
            for c4 in range(4):
                c = cg * 4 + c4
                s = self.wload("c%d_in" % ic, c, NK * 384)
                w = self.ws[s][:, 0:NK * 384].rearrange("p (k c) -> p k c", k=NK)
                for j in range(3):
                    for tt in range(NTT):
                        sl = slice(tt * TT, (tt + 1) * TT)
                        b = self.bank()
                        for k in range(NK):
                            self.mm(b, w[:, k, j * 128:(j + 1) * 128], self.hT[:, k, sl], k == 0, k == NK - 1,
                                    [("ws", s), ("hT", k, tt)])
                        psb = self.ps[b]
                        pout = pre[:, 1 + tt * TT: 1 + (tt + 1) * TT]
                        P.op("act", lambda e, pout=pout, psb=psb: e.activation(out=pout, in_=psb[:], func=AF.Copy),
                             reads=[("ps", b)], writes=[("pre", tt)])
                    q = j * 8 + c
                    w0 = csw[:, 0 * 24 + q: 0 * 24 + q + 1]
                    w1 = csw[:, 1 * 24 + q: 1 * 24 + q + 1]
                    w2 = csw[:, 2 * 24 + q: 2 * 24 + q + 1]
                    bb = csw[:, 72 + q: 72 + q + 1]
                    if j == 0:
                        dst, dkey = x2T[:, c, :], (lambda tt, c=c: ("gT", c, tt))
                    elif j == 1:
                        dst, dkey = x1T[:, c4, :], (lambda tt, c4=c4: ("x1T", c4, tt))
                    else:
                        dst, dkey = x2T[:, c, :], (lambda tt, c=c: ("gT", c, tt))
                    prk = [("pre", tt) for tt in range(NTT)] + [("pre", "h0"), ("pre", "h1"), "csw"]
                    for hf in range(2):
                        o0 = hf * 1024
                        P.op("dve", lambda e, o0=o0, w1=w1, bb=bb: e.tensor_scalar(
                            out=t1, in0=pre[:, 1 + o0:1 + o0 + 1024], scalar1=w1, scalar2=bb, op0=ALU.mult, op1=ALU.add),
                            reads=prk, writes=["t1"])
                        P.op("dve", lambda e, o0=o0, w0=w0: e.scalar_tensor_tensor(
                            out=t1, in0=pre[:, o0:o0 + 1024], scalar=w0, in1=t1, op0=ALU.mult, op1=ALU.add),
                            reads=prk + ["t1"], writes=["t1"])
                        dsl = dst[:, o0:o0 + 1024]
                        P.op("dve", lambda e, o0=o0, w2=w2, dsl=dsl: e.scalar_tensor_tensor(
                            out=dsl, in0=pre[:, 2 + o0:2 + o0 + 1024], scalar=w2, in1=t1, op0=ALU.mult, op1=ALU.add),
                            reads=prk + ["t1"], writes=[dkey(hf * 2), dkey(hf * 2 + 1)])
                    if j == 0:
                        transposes_to_tok(x2T[:, c, :], (lambda g4, c=c: ("gT", c, g4)), c4)
            P.barrier()
            long_conv(0, cg, x1T, lambda cc, tt: ("x1T", cc, tt))
            P.barrier()
            for c4 in range(4):
                transposes_to_tok(x1T[:, c4, :], (lambda g4, c4=c4: ("x1T", c4, g4)), c4)
            long_conv(1, cg, x2T[:, cg * 4:(cg + 1) * 4, :], lambda cc, tt, cg=cg: ("gT", cg * 4 + cc, tt))
        P.barrier()
        self.proj_accum_x("c%d_out" % ic, self.gT, "gT", list(range(NK)))

    def build(self):
        nc, P = self.nc, self.P
        with ExitStack() as st:
            self.alloc()
            P.dma("sp", self.gvec[:], self.dram["gvec"], writes=["gvec"])
            P.dma("sp", self.onesmean[:], self.dram["onesmean"], writes=["onesmean"])
            P.dma("sp", self.ident[:], self.dram["ident"], writes=["ident"])
            P.dma("sp", self.headones[:], self.dram["headones"], writes=["headones"])
            P.dma("sp", self.ones128[:], self.dram["ones128"], writes=["ones128"])
            self.hyena_on = True
            if 2 in self.layers and self.hyena_on:
                self.hyena_filter(0)
                P.barrier()
            for seq in range(self.nseq):
                for k in range(NK):
                    P.dma("sp", self.xT[:, k, :], self.x_in[seq, :, k, :], writes=self.tkeys("xT", k))
                ia = ib = ic = 0
                for l in range(DEPTH):
                    kind = l % 3
                    if l in self.layers:
                        P.barrier()
                        if kind == 0:
                            self.mixer_a(l, ia)
                        elif kind == 1:
                            self.mixer_b(l, ib)
                        elif self.hyena_on and os.environ.get("HY_STAGE") not in ("1", "2"):
                            self.mixer_c(l, ic)
                        P.barrier()
                        if not os.environ.get("HY_NOFFN"):
                            self.ffn(l)
                        P.barrier()
                    ia += kind == 0
                    ib += kind == 1
                    ic += kind == 2
                for k in range(NK):
                    P.dma("sp", self.y_out[seq, :, k, :], self.xT[:, k, :], reads=self.tkeys("xT", k))
            P.emit(nc, st)
        return nc


NEG = -30000.0


def consts():
    c = {}
    c["onesmean"] = np.full((128, 128), 1.0 / D, np.float32).astype(ml_dtypes.bfloat16)
    c["ident"] = np.eye(128, dtype=np.float32).astype(ml_dtypes.bfloat16)
    ho = np.zeros((128, 128), np.float32)
    ho[:64, :64] = 1.0 / 64
    ho[64:, 64:] = 1.0 / 64
    c["headones"] = ho.astype(ml_dtypes.bfloat16)
    kr = np.arange(2)[:, None, None, None]
    kc = np.arange(64)[None, :, None, None]
    qr = np.arange(2)[None, None, :, None]
    qc = np.arange(64)[None, None, None, :]
    cs = np.clip(qc - 8, 0, 48)
    colok = (kc >= cs) & (kc < cs + 16) & (kr >= 0) & (qr >= 0)
    mk = np.zeros((3, 2, 64, 2, 64), np.float32)
    mk[0] = np.where(colok, 0.0, NEG)
    dr = 4 + kr - qr
    mk[1] = np.where(colok & (dr >= -4) & (dr <= 3), 0.0, NEG)
    dr = -4 + kr - qr
    mk[2] = np.where(colok & (dr >= -4) & (dr <= 3), 0.0, NEG)
    c["namask"] = np.ascontiguousarray(mk.reshape(3, 128, 128).transpose(1, 0, 2)).reshape(128, 384).astype(np.float32)
    NF = 4096
    t = np.arange(L, dtype=np.int64)[:, None]
    f = np.arange(L, dtype=np.int64)[None, :]
    ph = ((2 * f + 1) * t) % (2 * NF)
    ang = ph.astype(np.float64) * (2.0 * np.pi / (2 * NF))
    Fm = np.concatenate([np.cos(ang), np.sin(ang)], axis=1).astype(ml_dtypes.bfloat16)
    Fr = Fm.reshape(16, 128, 32, 128)
    fwd = np.zeros((16, 128, 2, 16, 128), ml_dtypes.bfloat16)
    for i in range(16):
        fwd[i, :, 0] = Fr[:, :, i, :].transpose(1, 0, 2)
        fwd[i, :, 1] = Fr[:, :, 16 + i, :].transpose(1, 0, 2)
    c["dft_f"] = fwd.reshape(16, 128, 4096)
    Gi = Fm.reshape(4, 512, 4, 8, 128)
    c["dft_i"] = np.ascontiguousarray(Gi.transpose(0, 2, 4, 3, 1)).reshape(16, 128, 4096)
    f32 = np.float32
    tl = np.linspace(0.0, 1.0, L, dtype=f32)[:, None]
    bands = np.linspace(1e-4, 16 - 1, 16, dtype=f32)
    ang2 = (f32(2.0 * math.pi) * np.arange(L, dtype=f32)[:, None] / f32(L) * bands).astype(f32)
    z = np.concatenate([tl, np.cos(ang2), -np.sin(ang2)], axis=-1).astype(f32)
    c["hy_zT"] = np.ascontiguousarray(z.T)
    c["hy_negt"] = np.ascontiguousarray((-tl[:, 0]).reshape(16, 128).T).astype(f32)
    lt = math.log(1e-2)
    deltas = np.abs(np.linspace(lt / 1.5, lt / 0.3, D, dtype=f32)).astype(f32)
    c["hy_delta"] = np.ascontiguousarray(np.broadcast_to(deltas.reshape(2, 1, 512), (2, 128, 512))).astype(f32)
    c["ones128"] = np.ones((128, 128), np.float32).astype(ml_dtypes.bfloat16)
    return c


_DT = {np.dtype(np.float32): F32, np.dtype(ml_dtypes.bfloat16): BF16}


def run(inputs, layers=(0, 1, 2, 3), n_cores=N_CORES, nseq=SEQ_PER_CORE, trace=False):
    inp = {k: np.asarray(v) for k, v in inputs.items()}
    shared = prep_shared(inp)
    shared.update(consts())
    shapes = {k: (v.shape, _DT[v.dtype]) for k, v in shared.items()}
    bld = Builder(shapes, list(layers), nseq)
    nc = bld.build()
    x = inp["x"].astype(np.float32)
    in_maps = []
    for c in range(n_cores):
        xs = x[c * nseq:(c + 1) * nseq]
        xl = np.ascontiguousarray(xs.reshape(nseq, L, NK, 128).transpose(0, 3, 2, 1))
        m = dict(shared)
        m["x"] = xl
        in_maps.append(m)
    res = run_bass_kernel_spmd(nc, in_maps, core_ids=list(range(n_cores)), trace=trace)
    outs = []
    for c in range(n_cores):
        y = res.results[c]["y"]
        outs.append(np.ascontiguousarray(y.transpose(0, 3, 2, 1)).reshape(nseq, L, D))
    out = np.concatenate(outs, axis=0).astype(np.float32)
    return out, res


def kernel(**inputs):
    out, _ = run(inputs)
    return out
```

```python
import math
import os
from contextlib import ExitStack
import numpy as np
import ml_dtypes
import concourse.bass as bass
import concourse.mybir as mybir
from concourse.bass_utils import run_bass_kernel_spmd

F32 = mybir.dt.float32
BF16 = mybir.dt.bfloat16
AF = mybir.ActivationFunctionType
ALU = mybir.AluOpType

D = 1024
L = 2048
NK = 8
NTT = 4
TT = 512
DEPTH = 4
FFH = 2816
NHC = 22
EPS = 1e-6
N_CORES = 8
SEQ_PER_CORE = 2

COMPUTE = ("pe", "act", "dve", "pool")
QUEUES = ("sp",)
N_STREAMS = 8
SAME_ENG_SYNC = bool(int(os.environ.get("SES", "0")))


class Prog:
    def __init__(self):
        self.ops = {e: [] for e in COMPUTE + QUEUES}
        self.last_write = {}
        self.reads_since = {}
        self.stream_rr = {"sp": 0, "pool": 0, "act": 0}
        self.stream_cnt = {}
        self.pending = {}

    def _deps(self, me, reads, writes):
        deps = set()
        for r in reads:
            if r in self.last_write:
                deps.add(self.last_write[r])
        for w in writes:
            if w in self.last_write:
                deps.add(self.last_write[w])
            for x in self.reads_since.get(w, ()):
                deps.add(x)
        for w in writes:
            self.reads_since[w] = []
            self.last_write[w] = me
        for r in reads:
            if r not in writes:
                self.reads_since.setdefault(r, []).append(me)
        deps.discard(me)
        return deps

    def op(self, eng, fn, reads=(), writes=()):
        idx = len(self.ops[eng])
        me = ("e", eng, idx)
        deps = self._deps(me, tuple(reads), tuple(writes))
        deps |= self.pending.pop(eng, set())
        self.ops[eng].append(dict(kind="op", fn=fn, deps=deps))
        return me

    def dma(self, issuer, out, in_, reads=(), writes=()):
        s = self.stream_rr[issuer]
        self.stream_rr[issuer] = (s + 1) % N_STREAMS
        key = (issuer, s)
        k = self.stream_cnt.get(key, 0)
        self.stream_cnt[key] = k + 1
        me = ("d", issuer, s, k)
        deps = self._deps(me, tuple(reads), tuple(writes))
        if k > 0:
            deps.add(("d", issuer, s, k - 1))
        deps |= self.pending.pop(issuer, set())
        self.ops[issuer].append(dict(kind="dma", out=out, in_=in_, deps=deps, stream=s))
        return me

    def barrier(self, engines=("pe", "act", "dve"), extra=()):
        front = {}
        for e in engines:
            j = len(self.ops[e]) - 1
            while j >= 0 and self.ops[e][j]["kind"] != "op":
                j -= 1
            if j >= 0:
                front[e] = ("e", e, j)
        for e in tuple(engines) + ("sp",) + tuple(extra):
            for e2, d in front.items():
                if e2 != e:
                    self.pending.setdefault(e, set()).add(d)

    def _skip(self, e, d):
        return d[1] == e and (e == "pe" or e in QUEUES or not SAME_ENG_SYNC)

    def emit(self, nc, stack):
        engs = COMPUTE + QUEUES
        signal = {e: set() for e in engs}
        for e in engs:
            for o in self.ops[e]:
                for d in o["deps"]:
                    if d[0] == "e" and not self._skip(e, d):
                        signal[d[1]].add(d[2])
        count = {}
        for e in engs:
            c = 0
            for i, o in enumerate(self.ops[e]):
                if o["kind"] == "op" and i in signal[e]:
                    c += 1
                    count[(e, i)] = c
        sem = {e: stack.enter_context(nc.semaphore("s_" + e)) for e in engs}
        dsem = {}
        for (issuer, s) in self.stream_cnt:
            dsem[(issuer, s)] = stack.enter_context(nc.semaphore("d_%s%d" % (issuer, s)))
        block = stack.enter_context(nc.Block())
        prog = self

        def run(e, engine):
            waited = {}
            for i, o in enumerate(prog.ops[e]):
                for d in sorted(o["deps"], key=str):
                    if d[0] == "e":
                        if prog._skip(e, d):
                            continue
                        sm, val = sem[d[1]], count[(d[1], d[2])]
                        k = ("e", d[1])
                    else:
                        sm, val = dsem[(d[1], d[2])], 16 * (d[3] + 1)
                        k = ("d", d[1], d[2])
                    if waited.get(k, 0) >= val:
                        continue
                    waited[k] = val
                    engine.wait_ge(sm, val)
                if o["kind"] == "dma":
                    ins = engine.dma_start(out=o["out"], in_=o["in_"])
                    ins.then_inc(dsem[(e, o["stream"])], 16)
                else:
                    ins = o["fn"](engine)
                    if (e, i) in count:
                        ins.then_inc(sem[e], 1)

        def finish(engine):
            for (issuer, st), n in prog.stream_cnt.items():
                engine.wait_ge(dsem[(issuer, st)], 16 * n)

        @block.tensor
        def _(eng):
            run("pe", eng)

        @block.scalar
        def _(eng):
            run("act", eng)

        @block.vector
        def _(eng):
            run("dve", eng)

        @block.gpsimd
        def _(eng):
            run("pool", eng)

        @block.sync
        def _(eng):
            run("sp", eng)
            finish(eng)


def _units(W, col_lists):
    Wr = W.reshape(NK, 128, W.shape[1])
    out = []
    for cols in col_lists:
        out.append(np.ascontiguousarray(Wr[:, :, cols].transpose(1, 0, 2)).reshape(128, -1))
    return np.stack(out).astype(np.float32)


def _chunk_cols(*starts):
    return np.concatenate([np.arange(s, s + 128) for s in starts])


def _pvec(v):
    return np.ascontiguousarray(v.reshape(-1, 128).T).astype(np.float32)


def prep_shared(inp):
    sh = {}
    g = np.zeros((128, DEPTH * 2 * NK), np.float32)
    for l in range(DEPTH):
        g[:, (l * 2) * NK:(l * 2 + 1) * NK] = _pvec(inp["norm_mix_g"][l])
        g[:, (l * 2 + 1) * NK:(l * 2 + 2) * NK] = _pvec(inp["norm_ffn_g"][l])
    sh["gvec"] = g
    for l in range(DEPTH):
        w13 = inp["f_w13"][l]
        cols = [_chunk_cols(2 * i * 128, FFH + 2 * i * 128, (2 * i + 1) * 128, FFH + (2 * i + 1) * 128)
                for i in range(NHC // 2)]
        sh["f%d_up" % l] = _units(w13, cols)
        w2 = inp["f_w2"][l]
        dn = np.zeros((6, 128, NK, 512), np.float32)
        for grp in range(3):
            nh = min(8, NHC - grp * 8)
            for half in range(2):
                blk = w2[grp * 1024: grp * 1024 + nh * 128, half * 512:(half + 1) * 512]
                dn[grp * 2 + half, :, :nh, :] = blk.reshape(nh, 128, 512).transpose(1, 0, 2)
        sh["f%d_dn" % l] = dn.reshape(6, 128, NK * 512)
    for ia in range(inp["a_w_in"].shape[0]):
        cols = [_chunk_cols(m * 128, D + m * 128, 2 * D + m * 128) for m in range(NK)]
        sh["a%d_in" % ia] = _units(inp["a_w_in"][ia], cols)
        sh["a%d_out" % ia] = _units(inp["a_w_out"][ia], [np.arange(0, 512), np.arange(512, 1024)])
        cw = inp["a_conv_w"][ia]
        sh["a%d_cw" % ia] = np.concatenate([_pvec(cw[j]) for j in range(3)], axis=1)
    for ib in range(inp["b_w_qkv"].shape[0]):
        cols = [_chunk_cols(m * 128, D + m * 128, 2 * D + m * 128) for m in range(NK)]
        sh["b%d_in" % ib] = _units(inp["b_w_qkv"][ib], cols)
        sh["b%d_out" % ib] = _units(inp["b_w_out"][ib], [np.arange(0, 512), np.arange(512, 1024)])
        rpb = inp["b_rpb"][ib]
        kr = np.arange(2)[:, None, None, None]
        kc = np.arange(64)[None, :, None, None]
        qr = np.arange(2)[None, None, :, None]
        qc = np.arange(64)[None, None, None, :]
        dc = np.clip(kc - qc + 15, 0, 30) + 0 * kr + 0 * qr
        tabs = np.zeros((16, 7, 128, 128), np.float32)
        for oi in range(7):
            dr = 2 * (oi - 3) + kr - qr + 7 + 0 * kc + 0 * qc
            ok = (dr >= 0) & (dr <= 14)
            g = rpb[:, np.clip(dr, 0, 14), dc]
            g = np.where(ok[None], g, np.float32(0.0))
            tabs[:, oi] = g.reshape(16, 128, 128)
        sh["b%d_rpbT" % ib] = np.ascontiguousarray(
            tabs.reshape(8, 2, 7, 128, 128).transpose(0, 3, 1, 2, 4)).reshape(8, 128, 2 * 7 * 128)
        qkg = np.zeros((128, 2), np.float32)
        qkg[:, 0] = np.tile(inp["b_q_norm_g"][ib], 2)
        qkg[:, 1] = np.tile(inp["b_k_norm_g"][ib], 2)
        sh["b%d_qkg" % ib] = qkg
    for ic in range(inp["c_w_in"].shape[0]):
        cols = [_chunk_cols(c * 128, D + c * 128, 2 * D + c * 128) for c in range(NK)]
        sh["c%d_in" % ic] = _units(inp["c_w_in"][ic], cols)
        sh["c%d_out" % ic] = _units(inp["c_w_out"][ic], [np.arange(0, 512), np.arange(512, 1024)])
        sw = inp["c_short_w"][ic]
        sb = inp["c_short_b"][ic]
        cs = np.zeros((128, 96), np.float32)
        for tap in range(3):
            cs[:, tap * 24:(tap + 1) * 24] = _pvec(sw[tap])
        cs[:, 72:96] = _pvec(sb)
        sh["c%d_cs" % ic] = cs
        sh["c%d_w1" % ic] = np.ascontiguousarray(inp["c_f_w1"][ic]).astype(np.float32)
        sh["c%d_w2" % ic] = np.ascontiguousarray(inp["c_f_w2"][ic]).astype(np.float32)
        sh["c%d_w3" % ic] = np.ascontiguousarray(inp["c_f_w3"][ic]).astype(np.float32)
        fv = np.zeros((64, 4), np.float32)
        fv[:, 0] = inp["c_f_b1"][ic]
        fv[:, 1] = inp["c_f_b2"][ic]
        fv[:, 2] = inp["c_f_freq"][ic]
        sh["c%d_fv" % ic] = fv
        sh["c%d_skip" % ic] = np.ascontiguousarray(inp["c_f_skip"][ic]).astype(np.float32)
    return sh


class Builder:
    def __init__(self, shared_shapes, layers, nseq):
        self.layers = layers
        self.nseq = nseq
        nc = bass.Bass("TRN2", target_bir_lowering=False)
        self.nc = nc
        self.P = Prog()
        self.dram = {}
        for name, (shape, dt) in shared_shapes.items():
            self.dram[name] = nc.dram_tensor(name, list(shape), dt, kind="ExternalInput").ap()
        self.x_in = nc.dram_tensor("x", [nseq, 128, NK, L], F32, kind="ExternalInput").ap()
        self.y_out = nc.dram_tensor("y", [nseq, 128, NK, L], F32, kind="ExternalOutput").ap()
        self.bank_rr = 0
        self.slot_rr = 0
        self.uid = 0

    def alloc(self):
        nc = self.nc
        self.xT = nc.alloc_sbuf_tensor("sb_xT", [128, NK, L], F32)
        self.hT = nc.alloc_sbuf_tensor("sb_hT", [128, NK, L], BF16)
        self.gT = nc.alloc_sbuf_tensor("sb_gT", [128, NK, L], BF16)
        self.NSLOT = 3
        self.ws = [nc.alloc_sbuf_tensor("sb_ws%d" % i, [128, 4096], BF16) for i in range(self.NSLOT)]
        self.SCRB = 51328
        self.scr = nc.alloc_sbuf_tensor("sb_scr", [128, self.SCRB // 4], F32)
        self.gvec = nc.alloc_sbuf_tensor("sb_gvec", [128, DEPTH * 2 * NK], F32)
        self.onesmean = nc.alloc_sbuf_tensor("sb_onesmean", [128, 128], BF16)
        self.small = nc.alloc_sbuf_tensor("sb_small", [128, 64], F32)
        self.ident = nc.alloc_sbuf_tensor("sb_ident", [128, 128], BF16)
        self.headones = nc.alloc_sbuf_tensor("sb_headones", [128, 128], BF16)
        self.ones128 = nc.alloc_sbuf_tensor("sb_ones128", [128, 128], BF16)
        self.hsk = nc.alloc_sbuf_tensor("sb_hsk", [128, 16], F32)
        self.Hs = nc.dram_tensor("hs_scratch", [2, 2, 16, 128, 1024], BF16, kind="ExternalOutput").ap()
        self.ps = [nc.alloc_psum_tensor("ps%d" % i, [128, 512], F32) for i in range(8)]

    def scr_view(self, off, shape, dt):
        esz = 4 if dt == F32 else 2
        n = int(np.prod(shape[1:]))
        assert off % 4 == 0 and off + n * esz <= self.SCRB, (off, n, esz)
        ap = self.scr[:, off // 4: (off + n * esz) // 4]
        if dt != F32:
            ap = ap.bitcast(dt)
        if len(shape) == 3:
            ap = ap.rearrange("p (a b) -> p a b", a=shape[1])
        return ap

    def bank(self):
        b = self.bank_rr
        self.bank_rr = (b + 1) % 8
        return b

    def wload(self, name, u, ncols_total):
        s = self.slot_rr
        self.slot_rr = (s + 1) % self.NSLOT
        self.P.dma("pool", self.ws[s][:, 0:ncols_total], self.dram[name][u], writes=[("ws", s)])
        return s

    def mm(self, b, lhsT, rhs, start, stop, reads):
        ps = self.ps[b]
        n = rhs.shape[-1]
        m = lhsT.shape[-1]
        self.P.op("pe", lambda e: e.matmul(ps[0:m, 0:n], lhsT=lhsT, rhs=rhs, start=start, stop=stop, skip_group_check=True),
                  reads=reads, writes=[("ps", b)])

    def tkeys(self, name, k, tts=range(NTT)):
        return [(name, k, t) for t in tts]

    def rmsnorm(self, gcol):
        P = self.P
        sq = [self.scr_view(i * 1024, [128, TT], BF16) for i in range(2)]
        rt = self.scr_view(2048, [128, TT], F32)
        rstd = self.scr_view(4096, [128, TT], F32)
        for tt in range(NTT):
            sl = slice(tt * TT, (tt + 1) * TT)
            b = self.bank()
            for k in range(NK):
                q = sq[k % 2]
                xin = self.xT[:, k, sl]
                P.op("act", lambda e, q=q, xin=xin: e.activation(out=q, in_=xin, func=AF.Square),
                     reads=[("xT", k, tt)], writes=[("sq", k % 2)])
                self.mm(b, self.onesmean[:], q, k == 0, k == NK - 1, [("sq", k % 2), "onesmean"])
            psb = self.ps[b]
            P.op("act", lambda e, psb=psb: e.activation(out=rt, in_=psb[:], func=AF.Ln, bias=EPS, scale=1.0),
                 reads=[("ps", b)], writes=["rt"])
            P.op("act", lambda e: e.activation(out=rstd, in_=rt, func=AF.Exp, scale=-0.5), reads=["rt"], writes=["rstd"])
            for k in range(NK):
                xin = self.xT[:, k, sl]
                hout = self.hT[:, k, sl]
                gap = self.gvec[:, gcol + k: gcol + k + 1]
                P.op("dve", lambda e, xin=xin, hout=hout, gap=gap: e.scalar_tensor_tensor(
                    out=hout, in0=xin, scalar=gap, in1=rstd, op0=ALU.mult, op1=ALU.mult),
                    reads=[("xT", k, tt), "rstd", "gvec"], writes=[("hT", k, tt)])

    def proj_accum_x(self, wname, src, src_name, nkc_list):
        P = self.P
        for half in range(2):
            s = self.wload(wname, half, NK * 512) if not isinstance(wname, tuple) else self.wload(wname[0], wname[1] + half, NK * 512)
            w = self.ws[s][:, :].rearrange("p (k c) -> p k c", k=NK)
            for fo in range(4):
                kx = half * 4 + fo
                for tt in range(NTT):
                    sl = slice(tt * TT, (tt + 1) * TT)
                    b = self.bank()
                    for i, kc in enumerate(nkc_list):
                        self.mm(b, w[:, kc, fo * 128:(fo + 1) * 128], src[:, kc, sl], i == 0, i == len(nkc_list) - 1,
                                [("ws", s), (src_name, kc, tt)])
                    psb = self.ps[b]
                    xap = self.xT[:, kx, sl]
                    P.op("dve", lambda e, psb=psb, xap=xap: e.tensor_tensor(out=xap, in0=xap, in1=psb[:], op=ALU.add),
                         reads=[("ps", b), ("xT", kx, tt)], writes=[("xT", kx, tt)])

    def ffn(self, l):
        P = self.P
        self.rmsnorm((l * 2 + 1) * NK)
        aT = self.gT
        sg = [self.scr_view(8192 + i * 2048, [128, TT], F32) for i in range(2)]
        sgi = 0
        for grp in range(3):
            nh = min(8, NHC - grp * 8)
            for pair in range(nh // 2):
                u = grp * 4 + pair
                s = self.wload("f%d_up" % l, u, NK * 512)
                w = self.ws[s][:, :].rearrange("p (k c) -> p k c", k=NK)
                for hh in range(2):
                    hc = pair * 2 + hh
                    for tt in range(NTT):
                        sl = slice(tt * TT, (tt + 1) * TT)
                        bg, bu = self.bank(), self.bank()
                        for k in range(NK):
                            self.mm(bg, w[:, k, (2 * hh) * 128:(2 * hh + 1) * 128], self.hT[:, k, sl], k == 0, k == NK - 1,
                                    [("ws", s), ("hT", k, tt)])
                        for k in range(NK):
                            self.mm(bu, w[:, k, (2 * hh + 1) * 128:(2 * hh + 2) * 128], self.hT[:, k, sl], k == 0, k == NK - 1,
                                    [("ws", s), ("hT", k, tt)])
                        sgt = sg[sgi % 2]
                        sgk = ("sg", sgi % 2)
                        sgi += 1
                        psg, psu = self.ps[bg], self.ps[bu]
                        P.op("act", lambda e, sgt=sgt, psg=psg: e.activation(out=sgt, in_=psg[:], func=AF.Silu),
                             reads=[("ps", bg)], writes=[sgk])
                        aout = aT[:, hc, sl]
                        P.op("dve", lambda e, sgt=sgt, psu=psu, aout=aout: e.tensor_tensor(out=aout, in0=sgt, in1=psu[:], op=ALU.mult),
                             reads=[("ps", bu), sgk], writes=[("gT", hc, tt)])
            self.proj_accum_x(("f%d_dn" % l, grp * 2), aT, "gT", list(range(nh)))

    def mixer_a(self, l, ia):
        P = self.P
        self.rmsnorm((l * 2) * NK)
        cwt = self.small
        us = [self.scr_view(6144 + i * 2048, [128, TT], F32) for i in range(2)]
        cu = [self.scr_view(10240 + i * 8208, [128, L + 2], F32) for i in range(2)]
        Bs = [self.scr_view(26656 + i * 4096, [128, L], BF16) for i in range(2)]
        t1 = self.scr_view(34848, [128, L], F32)
        usi = 0
        for i in range(2):
            c = cu[i]
            P.op("dve", lambda e, c=c: e.memset(c[:, 0:1], 0.0), writes=[("cu", i, "h0")])
            P.op("dve", lambda e, c=c: e.memset(c[:, L + 1:L + 2], 0.0), writes=[("cu", i, "h1")])
        P.dma("sp", cwt[:, 0:24], self.dram["a%d_cw" % ia], writes=["cwt"])
        for m in range(NK):
            s = self.wload("a%d_in" % ia, m, NK * 384)
            w = self.ws[s][:, 0:NK * 384].rearrange("p (k c) -> p k c", k=NK)
            c = cu[m % 2]
            Bm = Bs[m % 2]
            for tt in range(NTT):
                sl = slice(tt * TT, (tt + 1) * TT)
                bb, bc, bu = self.bank(), self.bank(), self.bank()
                for j, b in enumerate((bb, bc, bu)):
                    for k in range(NK):
                        self.mm(b, w[:, k, j * 128:(j + 1) * 128], self.hT[:, k, sl], k == 0, k == NK - 1,
                                [("ws", s), ("hT", k, tt)])
                psb_, psc_, psu_ = self.ps[bb], self.ps[bc], self.ps[bu]
                bout = Bm[:, sl]
                P.op("act", lambda e, bout=bout, psb_=psb_: e.activation(out=bout, in_=psb_[:], func=AF.Copy),
                     reads=[("ps", bb)], writes=[("Bs", m % 2, tt)])
                ut = us[usi % 2]
                uk = ("us", usi % 2)
                usi += 1
                P.op("act", lambda e, ut=ut, psu_=psu_: e.activation(out=ut, in_=psu_[:], func=AF.Copy),
                     reads=[("ps", bu)], writes=[uk])
                cout = c[:, 1 + tt * TT: 1 + (tt + 1) * TT]
                P.op("dve", lambda e, cout=cout, psc_=psc_, ut=ut: e.tensor_tensor(out=cout, in0=ut, in1=psc_[:], op=ALU.mult),
                     reads=[("ps", bc), uk], writes=[("cu", m % 2, tt)])
            cuk = [("cu", m % 2, tt) for tt in range(NTT)] + [("cu", m % 2, "h0"), ("cu", m % 2, "h1")]
            w0 = cwt[:, 0 * 8 + m: 0 * 8 + m + 1]
            w1 = cwt[:, 1 * 8 + m: 1 * 8 + m + 1]
            w2 = cwt[:, 2 * 8 + m: 2 * 8 + m + 1]
            P.op("dve", lambda e, c=c, w1=w1: e.tensor_scalar(out=t1, in0=c[:, 1:L + 1], scalar1=w1, scalar2=None, op0=ALU.mult),
                 reads=cuk + ["cwt"], writes=["t1"])
            P.op("dve", lambda e, c=c, w0=w0: e.scalar_tensor_tensor(out=t1, in0=c[:, 0:L], scalar=w0, in1=t1, op0=ALU.mult, op1=ALU.add),
                 reads=cuk + ["cwt", "t1"], writes=["t1"])
            P.op("dve", lambda e, c=c, w2=w2: e.scalar_tensor_tensor(out=t1, in0=c[:, 2:L + 2], scalar=w2, in1=t1, op0=ALU.mult, op1=ALU.add),
                 reads=cuk + ["cwt", "t1"], writes=["t1"])
            gout = self.gT[:, m, :]
            P.op("dve", lambda e, gout=gout, Bm=Bm: e.tensor_tensor(out=gout, in0=t1, in1=Bm, op=ALU.mult),
                 reads=["t1"] + [("Bs", m % 2, tt) for tt in range(NTT)], writes=self.tkeys("gT", m))
        self.proj_accum_x("a%d_out" % ia, self.gT, "gT", list(range(NK)))


    @staticmethod
    def na_keys(j):
        if j <= 1:
            return list(range(0, 4))
        if j >= 14:
            return list(range(12, 16))
        return list(range(j - 2, j + 3))

    def mixer_b(self, l, ib):
        P = self.P
        self.rmsnorm((l * 2) * NK)
        sq = [self.scr_view(i * 1024, [128, TT], BF16) for i in range(2)]
        rt = self.scr_view(2048, [128, TT], F32)
        rstd = self.scr_view(4096, [128, TT], F32)
        QK = [self.scr_view(6144, [128, L], BF16), self.scr_view(10240, [128, L], BF16)]
        Vx = self.scr_view(14336, [128, 32, 128], BF16)
        stg = self.scr_view(22528, [128, 14, 128], F32)
        tab = self.scr_view(29696, [128, 18, 128], BF16)
        PT = [self.scr_view(34304 + i * 1536, [128, 768], BF16) for i in range(2)]
        R = [self.scr_view(37376 + i * 2048, [128, TT], F32) for i in range(2)]
        msk = self.scr_view(41472, [128, 3, 128], F32)
        rtB = self.scr_view(47232, [128, TT], F32)
        rstdB = self.scr_view(49280, [128, TT], F32)
        qkg = self.small[:, 32:34]
        P.dma("sp", msk, self.dram["namask"].rearrange("p (a b) -> p a b", a=3), writes=["msk"])
        P.dma("sp", qkg, self.dram["b%d_qkg" % ib], writes=["qkg"])
        Vx4 = Vx.rearrange("p (c h) f -> p c h f", h=2)
        P.op("dve", lambda e: e.memset(Vx4[:, :, 0, 64:128], 1.0), writes=["vx_ones_a"])
        P.op("dve", lambda e: e.memset(Vx4[:, :, 1, 0:64], 1.0), writes=["vx_ones_b"])
        J = {c: [j for j in range(16) if c in self.na_keys(j)] for c in range(16)}
        pti = 0
        ri = 0
        for m in range(NK):
            s = self.wload("b%d_in" % ib, m, NK * 384)
            w = self.ws[s][:, 0:NK * 384].rearrange("p (k c) -> p k c", k=NK)
            P.dma("sp", stg, self.dram["b%d_rpbT" % ib][m].rearrange("p (a b) -> p a b", a=14), writes=["stg"])
            for j in range(2):
                for tt in range(NTT):
                    sl = slice(tt * TT, (tt + 1) * TT)
                    b = self.bank()
                    for k in range(NK):
                        self.mm(b, w[:, k, j * 128:(j + 1) * 128], self.hT[:, k, sl], k == 0, k == NK - 1,
                                [("ws", s), ("hT", k, tt)])
                    psb = self.ps[b]
                    q = sq[(j * NTT + tt) % 2]
                    qk_ = ("sq", (j * NTT + tt) % 2)
                    P.op("act", lambda e, q=q, psb=psb: e.activation(out=q, in_=psb[:], func=AF.Square),
                         reads=[("ps", b)], writes=[qk_])
                    b2 = self.bank()
                    self.mm(b2, self.headones[:], q, True, True, [qk_, "headones"])
                    ps2 = self.ps[b2]
                    par = (j * NTT + tt) % 2
                    rt_, rstd_ = (rt, rstd) if par == 0 else (rtB, rstdB)
                    rtk, rsk = ("rt" if par == 0 else "rtB"), ("rstd" if par == 0 else "rstdB")
                    P.op("act", lambda e, ps2=ps2, rt_=rt_: e.activation(out=rt_, in_=ps2[:], func=AF.Ln, bias=EPS, scale=1.0),
                         reads=[("ps", b2)], writes=[rtk])
                    P.op("act", lambda e, rt_=rt_, rstd_=rstd_: e.activation(out=rstd_, in_=rt_, func=AF.Exp, scale=-0.5), reads=[rtk], writes=[rsk])
                    qout = QK[j][:, sl]
                    gap = qkg[:, j:j + 1]
                    P.op("dve", lambda e, qout=qout, psb=psb, gap=gap, rstd_=rstd_: e.scalar_tensor_tensor(
                        out=qout, in0=psb[:], scalar=gap, in1=rstd_, op0=ALU.mult, op1=ALU.mult),
                        reads=[("ps", b), rsk, "qkg"], writes=[("QK", j, tt)])
            for g4 in range(4):
                b = self.bank()
                for i in range(4):
                    kc = g4 * 4 + i
                    for k in range(NK):
                        psv = self.ps[b]
                        lhsT = self.hT[:, k, kc * 128:(kc + 1) * 128]
                        rhs = w[:, k, 256:384]
                        P.op("pe", lambda e, psv=psv, lhsT=lhsT, rhs=rhs, i=i, k=k: e.matmul(
                            psv[:, i * 128:(i + 1) * 128], lhsT=lhsT, rhs=rhs, start=(k == 0), stop=(k == NK - 1),
                            skip_group_check=True),
                            reads=[("ws", s), ("hT", k, kc // 4)], writes=[("ps", b)])
                psv = self.ps[b]
                pv3 = psv[:, :].rearrange("p (i f) -> p i f", i=4)
                oa = Vx4[:, g4 * 4:(g4 + 1) * 4, 0, 0:64]
                ob = Vx4[:, g4 * 4:(g4 + 1) * 4, 1, 64:128]
                P.op("act", lambda e, oa=oa, pv3=pv3: e.activation(out=oa, in_=pv3[:, :, 0:64], func=AF.Copy),
                     reads=[("ps", b)], writes=[("Vx", g4, 0)])
                P.op("act", lambda e, ob=ob, pv3=pv3: e.activation(out=ob, in_=pv3[:, :, 64:128], func=AF.Copy),
                     reads=[("ps", b)], writes=[("Vx", g4, 1)])
            blk_src = [(5, 1), (4, 0), (3, 0), (2, 0), (1, 2), (6, 0), (5, 0), (1, 0), (0, 0)]
            for h in range(2):
                for bi, (oi, mv) in enumerate(blk_src):
                    tout = tab[:, h * 9 + bi, :]
                    tin = stg[:, h * 7 + oi, :]
                    mk = msk[:, mv, :]
                    P.op("dve", lambda e, tout=tout, tin=tin, mk=mk: e.scalar_tensor_tensor(
                        out=tout, in0=tin, scalar=8.0, in1=mk, op0=ALU.mult, op1=ALU.add),
                        reads=["stg", "msk"], writes=[("tab", h)])
            for h in range(2):
                pb = 64 * h
                od = [0, 1, 2, 3]
                for g in od:
                    psg = self.ps[g]
                    P.op("dve", lambda e, psg=psg: e.memset(psg[:], 0.0), writes=[("ps", g)])
                pend = []

                def emit_pv(c, js, pt, ptk, h=h):
                    vx = Vx[:, c * 2 + h, :]
                    ji = 0
                    while ji < len(js):
                        g = js[ji] // 4
                        je = ji
                        while je + 1 < len(js) and js[je + 1] // 4 == g:
                            je += 1
                        cnt = je - ji + 1
                        psg = self.ps[g]
                        o0 = (js[ji] % 4) * 128
                        rhs = pt[:, ji * 128:(je + 1) * 128]
                        P.op("pe", lambda e, psg=psg, vx=vx, rhs=rhs, o0=o0, cnt=cnt: e.matmul(
                            psg[:, o0:o0 + cnt * 128], lhsT=vx, rhs=rhs, start=False, stop=False, skip_group_check=True),
                            reads=[ptk, ("Vx", c // 4, h), "vx_ones_a", "vx_ones_b"], writes=[("ps", g)])
                        ji = je + 1

                for c in range(16):
                    js = J[c]
                    jlo = js[0]
                    n = len(js) * 128
                    sb = [4, 5] if (pti % 2 == 0) else [6, 7]
                    pt = PT[pti % 2]
                    ptk = ("PT", pti % 2)
                    pti += 1
                    kT = QK[1][pb:pb + 64, c * 128:(c + 1) * 128]
                    segs = [(0, min(n, 512))] + ([(512, n)] if n > 512 else [])
                    for si, (a0, a1) in enumerate(segs):
                        pss = self.ps[sb[si]]
                        qT = QK[0][pb:pb + 64, jlo * 128 + a0: jlo * 128 + a1]
                        P.op("pe", lambda e, pss=pss, kT=kT, qT=qT, a0=a0, a1=a1: e.matmul(
                            pss[:, 0:a1 - a0], lhsT=kT, rhs=qT, start=True, stop=False, skip_group_check=True),
                            reads=[("QK", 0, t) for t in range(NTT)] + [("QK", 1, c // 4)], writes=[("ps", sb[si])])
                    interior = 4 <= c <= 11
                    if interior:
                        for si, (a0, a1) in enumerate(segs):
                            pss = self.ps[sb[si]]
                            tb = tab[:, h * 9: h * 9 + 5, :].rearrange("p a b -> p (a b)")[:, a0:a1]
                            P.op("pe", lambda e, pss=pss, tb=tb, a0=a0, a1=a1: e.matmul(
                                pss[:, 0:a1 - a0], lhsT=self.ident[:], rhs=tb, start=False, stop=True, skip_group_check=True),
                                reads=[("tab", h), "ident"], writes=[("ps", sb[si])])
                    else:
                        for ji, j in enumerate(js):
                            o = c - j
                            masked = (2 <= j <= 13) and abs(o) == 2
                            bi = {2: (0 if masked else 6), 1: 1, 0: 2, -1: 3, -2: (4 if masked else 7), 3: 5, -3: 8}[o]
                            col = ji * 128
                            si = col // 512
                            pss = self.ps[sb[si]]
                            tb = tab[:, h * 9 + bi, :]
                            cc = col - si * 512
                            P.op("pe", lambda e, pss=pss, tb=tb, cc=cc: e.matmul(
                                pss[:, cc:cc + 128], lhsT=self.ident[:], rhs=tb, start=False, stop=True, skip_group_check=True),
                                reads=[("tab", h), "ident"], writes=[("ps", sb[si])])
                    for si, (a0, a1) in enumerate(segs):
                        pss = self.ps[sb[si]]
                        pto = pt[:, a0:a1]
                        P.op("act", lambda e, pss=pss, pto=pto, a0=a0, a1=a1: e.activation(
                            out=pto, in_=pss[:, 0:a1 - a0], func=AF.Exp, scale=0.125),
                            reads=[("ps", sb[si])], writes=[ptk])
                    pend.append((c, js, pt, ptk))
                    if len(pend) > 1:
                        emit_pv(*pend.pop(0))
                while pend:
                    emit_pv(*pend.pop(0))
                for g in od:
                    psg = self.ps[g]
                    r = R[ri % 2]
                    rk = ("R", ri % 2)
                    ri += 1
                    dpb = 64 - pb
                    P.op("act", lambda e, psg=psg, r=r, dpb=dpb: e.activation(out=r[dpb:dpb + 64, :], in_=psg[dpb:dpb + 64, :], func=AF.Ln),
                         reads=[("ps", g)], writes=[rk])
                    P.op("act", lambda e, r=r, dpb=dpb: e.activation(out=r[dpb:dpb + 64, :], in_=r[dpb:dpb + 64, :], func=AF.Exp, scale=-1.0),
                         reads=[rk], writes=[rk])
                    gout = self.gT[pb:pb + 64, m, g * 512:(g + 1) * 512]
                    P.op("dve", lambda e, psg=psg, r=r, dpb=dpb, gout=gout, pb=pb: e.tensor_tensor(
                        out=gout, in0=psg[pb:pb + 64, :], in1=r[dpb:dpb + 64, :], op=ALU.mult),
                        reads=[("ps", g), rk], writes=[("gT", m, g)])
        self.proj_accum_x("b%d_out" % ib, self.gT, "gT", list(range(NK)))

    def sload(self, dram_ap, ncols=4096):
        s = self.slot_rr
        self.slot_rr = (s + 1) % self.NSLOT
        self.P.dma(os.environ.get("HY_SQ", "pool"), self.ws[s][:, 0:ncols], dram_ap, writes=[("ws", s)])
        return s

    def sin_rr(self, out, in_, tmp, tmp2, np_, n, rkeys, wkeys):
        P = self.P
        MAGIC = 12582912.0
        TWO_PI = 2.0 * math.pi
        P.op("dve", lambda e: e.tensor_scalar(out=tmp, in0=in_, scalar1=1.0 / TWO_PI, scalar2=MAGIC, op0=ALU.mult, op1=ALU.add),
             reads=rkeys, writes=["srr_t"])
        P.op("dve", lambda e: e.tensor_scalar(out=tmp2, in0=tmp, scalar1=MAGIC, scalar2=-TWO_PI, op0=ALU.subtract, op1=ALU.mult),
             reads=["srr_t"], writes=["srr_t2"])
        P.op("dve", lambda e: e.tensor_tensor(out=tmp, in0=in_, in1=tmp2, op=ALU.add),
             reads=rkeys + ["srr_t2", "srr_t"], writes=["srr_t"])
        P.op("act", lambda e: e.activation(out=out, in_=tmp, func=AF.Sin, scale=1.0 - 2e-6),
             reads=["srr_t"], writes=wkeys)

    def hyena_filter(self, ic):
        P = self.P
        INV = 2.0 / 4096.0
        hreg = self.hT[:, :, :].rearrange("p a b -> p (a b)").bitcast(F32)
        zT = hreg[0:33, 0:2048]
        h1 = hreg[0:64, 2048:4096]
        hTb = self.hT[:, :, :].rearrange("p a b -> p (a b)")
        h2 = hTb[0:64, 8192:10240]
        w3f = hTb[0:64, 12288:12800]
        w3r = hTb[0:64, 12800:13312]
        w1 = hreg[0:33, 7168:7232]
        w2 = hreg[0:64, 7232:7296]
        fv = hreg[0:64, 7296:7300]
        fb = hreg[0:64, 7300:7302]
        pre = hreg[0:64, 7424:7936]
        tA = hreg[0:64, 7936:8192]
        greg = self.gT[:, :, :].rearrange("p a b -> p (a b)")
        s_tok = greg[:, 0:8192].rearrange("p (a b) -> p a b", a=16)
        d_tok = greg[:, 8192:16384].rearrange("p (a b) -> p a b", a=16)
        sv = lambda off, shape, dt: self.scr_view(off, shape, dt)
        tmp = sv(0, [128, TT], F32)
        tmp2 = sv(2048, [128, TT], F32)
        dec = sv(4096, [128, TT], F32)
        ff = sv(6144, [128, TT], F32)
        rr = sv(8192, [128, TT], F32)
        ab = [sv(10240 + i * 1024, [128, TT], BF16) for i in range(2)]
        a1 = sv(12288, [128, TT], F32)
        rn2 = sv(14336, [128, TT], F32)
        nrn2 = sv(16384, [128, TT], F32)
        skip2 = sv(18432, [128, TT], F32)
        drow = sv(20480, [128, TT], F32)
        hb = [sv(22528 + i * 2048, [128, 2, TT], BF16) for i in range(2)]
        negt = sv(26624, [128, 16], F32)
        a3 = sv(28672, [128, TT], F32)
        P.dma("sp", zT, self.dram["hy_zT"], writes=["zT"])
        P.dma("sp", w1, self.dram["c%d_w1" % ic], writes=["fw1"])
        P.dma("sp", w2, self.dram["c%d_w2" % ic], writes=["fw2"])
        P.dma("sp", fv, self.dram["c%d_fv" % ic], writes=["fv"])
        P.dma("sp", negt, self.dram["hy_negt"], writes=["negt"])
        P.op("dve", lambda e: e.tensor_tensor(out=fb[:, 0:1], in0=fv[:, 0:1], in1=fv[:, 2:3], op=ALU.mult), reads=["fv"], writes=["fb0"])
        P.op("dve", lambda e: e.tensor_tensor(out=fb[:, 1:2], in0=fv[:, 1:2], in1=fv[:, 2:3], op=ALU.mult), reads=["fv"], writes=["fb1"])
        for li, (wl, src, dst, np_in) in enumerate(((w1, zT, h1, 33), (w2, h1, h2, 64))):
            for tt in range(NTT):
                sl = slice(tt * TT, (tt + 1) * TT)
                b = self.bank()
                psb = self.ps[b]
                rhs = src[:, sl]
                P.op("pe", lambda e, psb=psb, wl=wl, rhs=rhs: e.matmul(psb[0:64, :], lhsT=wl, rhs=rhs, start=True, stop=True, skip_group_check=True),
                     reads=["fw1", "fw2", "zT", ("h1", tt)], writes=[("ps", b)])
                fbi = fb[:, li:li + 1]
                P.op("act", lambda e, psb=psb, fbi=fbi: e.activation(out=pre, in_=psb[0:64, :], func=AF.Identity, bias=fbi, scale=fv[:, 2:3]),
                     reads=[("ps", b), "fv", "fb0", "fb1"], writes=["fpre"])
                self.sin_rr(dst[:, sl], pre, tmp[0:64, :], tmp2[0:64, :], 64, TT, ["fpre"], [("h%d" % (li + 1), tt)])
        import os
        if os.environ.get("HY_STAGE") == "1":
            return
        h2keys = [("h2", tt) for tt in range(NTT)]
        abi = 0
        hbi = 0
        for o in range(2):
            for cg in range(2):
                colf = o * 2048 + cg * 512
                colr = colf + 1024
                if not (os.environ.get("HY_NOW3") and (o, cg) != (0, 0)):
                    P.dma("pool", w3f, self.dram["c%d_w3" % ic][:, colf:colf + 512], writes=["w3f"])
                    P.dma("pool", w3r, self.dram["c%d_w3" % ic][:, colr:colr + 512], writes=["w3r"])
                P.dma("sp", drow, self.dram["hy_delta"][cg], writes=["drow"])
                skrow = self.dram["c%d_skip" % ic][o:o + 1, cg * 512:(cg + 1) * 512].partition_broadcast(128)
                P.dma("sp", skip2, skrow, writes=["skip2"])
                P.op("dve", lambda e: e.tensor_scalar(out=skip2, in0=skip2, scalar1=INV, scalar2=None, op0=ALU.mult),
                     reads=["skip2"], writes=["skip2"])
                if os.environ.get("HY_ONEIT") and (o, cg) != (0, 0):
                    return
                cut2 = os.environ.get("HY_CUT2", "") if (o, cg) != (0, 0) else ""
                if os.environ.get("HY_CUT") == "a" or cut2 == "a":
                    return
                bn = self.bank()
                psn = self.ps[bn]
                ntc = int(os.environ.get("HY_TC2", "16")) if (o, cg) != (0, 0) else 16
                for tc in range(ntc):
                    bf_, br_ = self.bank(), self.bank()
                    if bf_ == bn or br_ == bn:
                        bf_, br_ = self.bank(), self.bank()
                    psf, psr = self.ps[bf_], self.ps[br_]
                    lh = h2[:, tc * 128:(tc + 1) * 128]
                    P.op("pe", lambda e, psf=psf, lh=lh: e.matmul(psf[:], lhsT=lh, rhs=w3f, start=True, stop=True, skip_group_check=True),
                         reads=h2keys + ["w3f"], writes=[("ps", bf_)])
                    P.op("pe", lambda e, psr=psr, lh=lh: e.matmul(psr[:], lhsT=lh, rhs=w3r, start=True, stop=True, skip_group_check=True),
                         reads=h2keys + ["w3r"], writes=[("ps", br_)])
                    tcb = os.environ.get("HY_TCB", "") if (o, cg) != (0, 0) else ""
                    if tcb == "m":
                        continue
                    nt = negt[:, tc:tc + 1]
                    P.op("act", lambda e, nt=nt: e.activation(out=dec, in_=drow, func=AF.Exp, scale=nt),
                         reads=["drow", "negt"], writes=["dec"])
                    P.op("dve", lambda e, psf=psf: e.tensor_tensor(out=ff, in0=psf[:], in1=dec, op=ALU.mult),
                         reads=[("ps", bf_), "dec"], writes=["ff"])
                    P.op("dve", lambda e, psr=psr: e.tensor_tensor(out=rr, in0=psr[:], in1=dec, op=ALU.mult),
                         reads=[("ps", br_), "dec"], writes=["rr"])
                    if tcb == "e":
                        continue
                    so = s_tok[:, tc, :]
                    do = d_tok[:, tc, :]
                    P.op("dve", lambda e, so=so: e.tensor_tensor(out=so, in0=ff, in1=rr, op=ALU.add),
                         reads=["ff", "rr"], writes=[("s_tok", tc)])
                    P.op("dve", lambda e, do=do: e.tensor_tensor(out=do, in0=ff, in1=rr, op=ALU.subtract),
                         reads=["ff", "rr"], writes=[("d_tok", tc)])
                    if tcb == "s":
                        continue
                    a2 = ab[abi % 2]
                    ak = ("ab", abi % 2)
                    abi += 1
                    P.op("act", lambda e: e.activation(out=a1, in_=ff, func=AF.Abs), reads=["ff"], writes=["a1"])
                    P.op("act", lambda e: e.activation(out=a3, in_=rr, func=AF.Abs), reads=["rr"], writes=["a3"])
                    P.op("dve", lambda e, a2=a2: e.tensor_tensor(out=a2, in0=a1, in1=a3, op=ALU.add),
                         reads=["a3", "a1"], writes=[ak])
                    if tc == 0:
                        P.op("dve", lambda e: e.tensor_tensor(out=a1[0:1, :], in0=ff[0:1, :], in1=rr[0:1, :], op=ALU.add),
                             reads=["ff", "rr", ak], writes=["a1"])
                        P.op("dve", lambda e, a2=a2: e.scalar_tensor_tensor(out=a2[0:1, :], in0=a1[0:1, :], scalar=-1.0, in1=a1[0:1, :], op0=ALU.mult, op1=ALU.max),
                             reads=["a1"], writes=[ak])
                    P.op("pe", lambda e, a2=a2, tc=tc, psn=psn: e.matmul(psn[:], lhsT=self.ones128[:], rhs=a2, start=(tc == 0), stop=(tc == 15), skip_group_check=True),
                         reads=[ak, "ones128"], writes=[("ps", bn)])
                if os.environ.get("HY_CUT") == "b" or cut2 == "b" or ntc < 16:
                    return
                P.op("dve", lambda e, psn=psn: e.reciprocal(out=rn2, in_=psn[:]), reads=[("ps", bn)], writes=["rn2"])
                P.op("dve", lambda e: e.tensor_scalar(out=rn2, in0=rn2, scalar1=INV, scalar2=None, op0=ALU.mult), reads=["rn2"], writes=["rn2"])
                P.op("dve", lambda e: e.tensor_scalar(out=nrn2, in0=rn2, scalar1=-1.0, scalar2=None, op0=ALU.mult), reads=["rn2"], writes=["nrn2"])
                if os.environ.get("HY_CUT") == "d" or cut2 == "d":
                    return
                for i in range(int(os.environ.get("HY_NP", "16"))):
                    s = self.sload(self.dram["dft_f"][i])
                    Fu = self.ws[s][:, :].rearrange("p (a t r) -> p a t r", a=2, t=16)
                    if os.environ.get("HY_CUT") == "e":
                        continue
                    bA, bB = self.bank(), self.bank()
                    for tc in range(16):
                        self.mm(bA, Fu[:, 0, tc, :], s_tok[:, tc, :], tc == 0, tc == 15, [("ws", s), ("s_tok", tc)])
                    for tc in range(16):
                        self.mm(bB, Fu[:, 1, tc, :], d_tok[:, tc, :], tc == 0, tc == 15, [("ws", s), ("d_tok", tc)])
                    psA, psB = self.ps[bA], self.ps[bB]
                    hbt = hb[hbi % 2]
                    hk = ("hb", hbi % 2)
                    hbi += 1
                    if os.environ.get("HY_CUT") == "f":
                        continue
                    P.op("dve", lambda e, psA=psA: e.tensor_tensor(out=tmp, in0=psA[:], in1=rn2, op=ALU.mult),
                         reads=[("ps", bA), "rn2"], writes=["srr_t"])
                    P.op("dve", lambda e, hbt=hbt: e.tensor_tensor(out=hbt[:, 0, :], in0=tmp, in1=skip2, op=ALU.add),
                         reads=["srr_t", "skip2"], writes=[(hk, 0)])
                    P.op("dve", lambda e, hbt=hbt, psB=psB: e.tensor_tensor(out=hbt[:, 1, :], in0=psB[:], in1=nrn2, op=ALU.mult),
                         reads=[("ps", bB), "nrn2"], writes=[(hk, 1)])
                    if os.environ.get("HY_CUT") != "c":
                        P.dma("sp", self.Hs[o, cg, i], hbt.rearrange("p a b -> p (a b)"), reads=[(hk, 0), (hk, 1)], writes=[("Hs", o, cg, i)])
                pass

    def mixer_c(self, l, ic):
        P = self.P
        sv = self.scr_view
        t1 = sv(0, [128, 1024], F32)
        csw = sv(4096, [128, 96], F32)
        x1T = sv(6144, [128, 4, L], BF16)
        tok = sv(22528, [128, 16, TT], BF16)
        pre = sv(38912, [128, L + 2], F32)
        hb = [sv(38912 + i * 2048, [128, 2, TT], BF16) for i in range(2)]
        tm = [sv(43008 + i * 2048, [128, TT], F32) for i in range(2)]
        Z = self.hT[:, :, :].rearrange("p a b -> p (a b)").rearrange("p (a b) -> p a b", a=32)
        x2T = self.gT
        hbi = 0

        def transposes_to_tok(src, src_keyfn, c4):
            for g4 in range(4):
                b = self.bank()
                psT = self.ps[b][:, :].bitcast(BF16)
                for i in range(4):
                    tc = g4 * 4 + i
                    sin = src[:, tc * 128:(tc + 1) * 128]
                    P.op("pe", lambda e, psT=psT, sin=sin, i=i: e.transpose(out=psT[:, i * 128:(i + 1) * 128], in_=sin, identity=self.ident[:]),
                         reads=[src_keyfn(g4), "ident"], writes=[("ps", b)])
                tout = tok[:, g4 * 4:(g4 + 1) * 4, c4 * 128:(c4 + 1) * 128]
                P.op("act", lambda e, tout=tout, psT=psT: e.activation(out=tout, in_=psT[:, 0:512].rearrange("p (i f) -> p i f", i=4), func=AF.Copy),
                     reads=[("ps", b)], writes=[("tok", g4, c4)])

        def long_conv(o, cg, gate, gate_keyfn):
            nonlocal hbi
            for i in range(16):
                s = self.sload(self.dram["dft_f"][i])
                Fu = self.ws[s][:, :].rearrange("p (a t r) -> p a t r", a=2, t=16)
                hbt = hb[hbi % 2]
                hk = ("hbm", hbi % 2)
                hbi += 1
                P.dma("sp", hbt.rearrange("p a b -> p (a b)"), self.Hs[o, cg, i], reads=[("Hs", o, cg, i)], writes=[hk])
                bA, bB = self.bank(), self.bank()
                tokk = [("tok", g4, c4) for g4 in range(4) for c4 in range(4)]
                for tc in range(16):
                    self.mm(bA, Fu[:, 0, tc, :], tok[:, tc, :], tc == 0, tc == 15, [("ws", s)] + (tokk if tc == 0 else []))
                for tc in range(16):
                    self.mm(bB, Fu[:, 1, tc, :], tok[:, tc, :], tc == 0, tc == 15, [("ws", s)])
                psA, psB = self.ps[bA], self.ps[bB]
                Hr, Hi = hbt[:, 0, :], hbt[:, 1, :]
                z1, z2 = Z[:, i, :], Z[:, 16 + i, :]
                P.op("dve", lambda e, psA=psA, Hr=Hr: e.tensor_tensor(out=tm[0], in0=psA[:], in1=Hr, op=ALU.mult),
                     reads=[("ps", bA), hk], writes=[("tm", 0)])
                P.op("dve", lambda e, psB=psB, Hi=Hi: e.tensor_tensor(out=tm[1], in0=psB[:], in1=Hi, op=ALU.mult),
                     reads=[("ps", bB), hk], writes=[("tm", 1)])
                P.op("dve", lambda e, z1=z1: e.tensor_tensor(out=z1, in0=tm[0], in1=tm[1], op=ALU.add),
                     reads=[("tm", 0), ("tm", 1)], writes=[("Z", i)])
                P.op("dve", lambda e, psB=psB, Hr=Hr: e.tensor_tensor(out=tm[0], in0=psB[:], in1=Hr, op=ALU.mult),
                     reads=[("ps", bB), hk, ("tm", 0)], writes=[("tm", 0)])
                P.op("dve", lambda e, psA=psA, Hi=Hi: e.tensor_tensor(out=tm[1], in0=psA[:], in1=Hi, op=ALU.mult),
                     reads=[("ps", bA), hk, ("tm", 1)], writes=[("tm", 1)])
                P.op("dve", lambda e, z2=z2: e.tensor_tensor(out=z2, in0=tm[0], in1=tm[1], op=ALU.subtract),
                     reads=[("tm", 0), ("tm", 1)], writes=[("Z", 16 + i)])
            Tb = [sv(47104 + i2 * 1024, [128, TT], BF16) for i2 in range(2)]
            for i in range(8):
                for (ra, rb, first, second, ti) in ((i, 8 + i, ALU.subtract, ALU.add, 0), (16 + i, 24 + i, ALU.add, ALU.subtract, 1)):
                    za, zb = Z[:, ra, :], Z[:, rb, :]
                    T = Tb[ti]
                    tk = ("Tb", ti)
                    P.op("dve", lambda e, za=za, zb=zb, T=T, first=first: e.tensor_tensor(out=T, in0=za, in1=zb, op=first),
                         reads=[("Z", ra), ("Z", rb)], writes=[tk])
                    P.op("dve", lambda e, za=za, zb=zb, second=second: e.tensor_tensor(out=za, in0=za, in1=zb, op=second),
                         reads=[("Z", ra), ("Z", rb)], writes=[("Z", ra)])
                    P.op("act", lambda e, zb=zb, T=T: e.activation(out=zb, in_=T, func=AF.Copy),
                         reads=[tk, ("Z", rb)], writes=[("Z", rb)])
            for par in range(2):
                for tile in range(2):
                    acc = [self.bank() for _ in range(4)]
                    for rg in range(2):
                        s = self.sload(self.dram["dft_i"][(par * 2 + tile) * 2 + rg])
                        G = self.ws[s][:, :].rearrange("p (a t) -> p a t", a=8)
                        for ci in range(8):
                            fc = (0 if rg == 0 else 16) + (0 if par == 0 else 8) + ci
                            first = (rg == 0 and ci == 0)
                            last = (rg == 1 and ci == 7)
                            for cc in range(4):
                                self.mm(acc[cc], Z[:, fc, cc * 128:(cc + 1) * 128], G[:, ci, :], first, last, [("ws", s), ("Z", fc)])
                    for cc in range(4):
                        psy = self.ps[acc[cc]]
                        gap = gate[:, cc, par + 1024 * tile: par + 1024 * tile + 1023: 2]
                        gk = [gate_keyfn(cc, 2 * tile), gate_keyfn(cc, 2 * tile + 1)]
                        P.op("dve", lambda e, psy=psy, gap=gap: e.tensor_tensor(out=gap, in0=gap, in1=psy[:], op=ALU.mult),
                             reads=[("ps", acc[cc])] + gk, writes=gk)

        for cg in range(2):
            P.barrier()
            self.rmsnorm((l * 2) * NK)
            P.barrier()
            P.dma("sp", csw, self.dram["c%d_cs" % ic], writes=["csw"])
            P.op("dve", lambda e: e.memset(pre[:, 0:1], 0.0), writes=[("pre", "h0")])
            P.op("dve", lambda e: e.memset(pre[:, L + 1:L + 2], 0.0), writes=[("pre", "h1")])
            for c4 in range(4):
                c = cg * 4 + c4
                s = self.wload("c%d_in" % ic, c, NK * 384)
                w = self.ws[s][:, 0:NK * 384].rearrange("p (k c) -> p k c", k=NK)
                for j in range(3):
                    for tt in range(NTT):
                        sl = slice(tt * TT, (tt + 1) * TT)
                        b = self.bank()
                        for k in range(NK):
                            self.mm(b, w[:, k, j * 128:(j + 1) * 128], self.hT[:, k, sl], k == 0, k == NK - 1,
                                    [("ws", s), ("hT", k, tt)])
                        psb = self.ps[b]
                        pout = pre[:, 1 + tt * TT: 1 + (tt + 1) * TT]
                        P.op("act", lambda e, pout=pout, psb=psb: e.activation(out=pout, in_=psb[:], func=AF.Copy),
                             reads=[("ps", b)], writes=[("pre", tt)])
                    q = j * 8 + c
                    w0 = csw[:, 0 * 24 + q: 0 * 24 + q + 1]
                    w1 = csw[:, 1 * 24 + q: 1 * 24 + q + 1]
                    w2 = csw[:, 2 * 24 + q: 2 * 24 + q + 1]
                    bb = csw[:, 72 + q: 72 + q + 1]
                    if j == 0:
                        dst, dkey = x2T[:, c, :], (lambda tt, c=c: ("gT", c, tt))
                    elif j == 1:
                        dst, dkey = x1T[:, c4, :], (lambda tt, c4=c4: ("x1T", c4, tt))
                    else:
                        dst, dkey = x2T[:, c, :], (lambda tt, c=c: ("gT", c, tt))
                    prk = [("pre", tt) for tt in range(NTT)] + [("pre", "h0"), ("pre", "h1"), "csw"]
                    for hf in range(2):
                        o0 = hf * 1024
                        P.op("dve", lambda e, o0=o0, w1=w1, bb=bb: e.tensor_scalar(
                            out=t1, in0=pre[:, 1 + o0:1 + o0 + 1024], scalar1=w1, scalar2=bb, op0=ALU.mult, op1=ALU.add),
                            reads=prk, writes=["t1"])
                        P.op("dve", lambda e, o0=o0, w0=w0: e.scalar_tensor_tensor(
                            out=t1, in0=pre[:, o0:o0 + 1024], scalar=w0, in1=t1, op0=ALU.mult, op1=ALU.add),
                            reads=prk + ["t1"], writes=["t1"])
                        dsl = dst[:, o0:o0 + 1024]
                        P.op("dve", lambda e, o0=o0, w2=w2, dsl=dsl: e.scalar_tensor_tensor(
                            out=dsl, in0=pre[:, 2 + o0:2 + o0 + 1024], scalar=w2, in1=t1, op0=ALU.mult, op1=ALU.add),
                            reads=prk + ["t1"], writes=[dkey(hf * 2), dkey(hf * 2 + 1)])
                    if j == 0:
                        transposes_to_tok(x2T[:, c, :], (lambda g4, c=c: ("gT", c, g4)), c4)
            P.barrier()
            long_conv(0, cg, x1T, lambda cc, tt: ("x1T", cc, tt))
            P.barrier()
            for c4 in range(4):
                transposes_to_tok(x1T[:, c4, :], (lambda g4, c4=c4: ("x1T", c4, g4)), c4)
            long_conv(1, cg, x2T[:, cg * 4:(cg + 1) * 4, :], lambda cc, tt, cg=cg: ("gT", cg * 4 + cc, tt))
        P.barrier()
        self.proj_accum_x("c%d_out" % ic, self.gT, "gT", list(range(NK)))

    def build(self):
        nc, P = self.nc, self.P
        with ExitStack() as st:
            self.alloc()
            P.dma("sp", self.gvec[:], self.dram["gvec"], writes=["gvec"])
            P.dma("sp", self.onesmean[:], self.dram["onesmean"], writes=["onesmean"])
            P.dma("sp", self.ident[:], self.dram["ident"], writes=["ident"])
            P.dma("sp", self.headones[:], self.dram["headones"], writes=["headones"])
            P.dma("sp", self.ones128[:], self.dram["ones128"], writes=["ones128"])
            self.hyena_on = True
            if 2 in self.layers and self.hyena_on:
                self.hyena_filter(0)
                P.barrier()
            for seq in range(self.nseq):
                for k in range(NK):
                    P.dma("sp", self.xT[:, k, :], self.x_in[seq, :, k, :], writes=self.tkeys("xT", k))
                ia = ib = ic = 0
                for l in range(DEPTH):
                    kind = l % 3
                    if l in self.layers:
                        P.barrier()
                        if kind == 0:
                            self.mixer_a(l, ia)
                        elif kind == 1:
                            self.mixer_b(l, ib)
                        elif self.hyena_on and os.environ.get("HY_STAGE") not in ("1", "2"):
                            self.mixer_c(l, ic)
                        P.barrier()
                        if not os.environ.get("HY_NOFFN"):
                            self.ffn(l)
                        P.barrier()
                    ia += kind == 0
                    ib += kind == 1
                    ic += kind == 2
                for k in range(NK):
                    P.dma("sp", self.y_out[seq, :, k, :], self.xT[:, k, :], reads=self.tkeys("xT", k))
            P.emit(nc, st)
        return nc


NEG = -30000.0


def consts():
    c = {}
    c["onesmean"] = np.full((128, 128), 1.0 / D, np.float32).astype(ml_dtypes.bfloat16)
    c["ident"] = np.eye(128, dtype=np.float32).astype(ml_dtypes.bfloat16)
    ho = np.zeros((128, 128), np.float32)
    ho[:64, :64] = 1.0 / 64
    ho[64:, 64:] = 1.0 / 64
    c["headones"] = ho.astype(ml_dtypes.bfloat16)
    kr = np.arange(2)[:, None, None, None]
    kc = np.arange(64)[None, :, None, None]
    qr = np.arange(2)[None, None, :, None]
    qc = np.arange(64)[None, None, None, :]
    cs = np.clip(qc - 8, 0, 48)
    colok = (kc >= cs) & (kc < cs + 16) & (kr >= 0) & (qr >= 0)
    mk = np.zeros((3, 2, 64, 2, 64), np.float32)
    mk[0] = np.where(colok, 0.0, NEG)
    dr = 4 + kr - qr
    mk[1] = np.where(colok & (dr >= -4) & (dr <= 3), 0.0, NEG)
    dr = -4 + kr - qr
    mk[2] = np.where(colok & (dr >= -4) & (dr <= 3), 0.0, NEG)
    c["namask"] = np.ascontiguousarray(mk.reshape(3, 128, 128).transpose(1, 0, 2)).reshape(128, 384).astype(np.float32)
    NF = 4096
    t = np.arange(L, dtype=np.int64)[:, None]
    forder = np.concatenate([np.arange(1024), 2047 - np.arange(1024)]).astype(np.int64)
    f = forder[None, :]
    ph = ((2 * f + 1) * t) % (2 * NF)
    ang = ph.astype(np.float64) * (2.0 * np.pi / (2 * NF))
    Fm = np.concatenate([np.cos(ang), np.sin(ang)], axis=1).astype(ml_dtypes.bfloat16)
    Fr = Fm.reshape(16, 128, 32, 128)
    fwd = np.zeros((16, 128, 2, 16, 128), ml_dtypes.bfloat16)
    for i in range(16):
        fwd[i, :, 0] = Fr[:, :, i, :].transpose(1, 0, 2)
        fwd[i, :, 1] = Fr[:, :, 16 + i, :].transpose(1, 0, 2)
    c["dft_f"] = fwd.reshape(16, 128, 4096)
    fb = np.arange(1024, dtype=np.int64)
    inv = np.zeros((2, 2, 2, 128, 8, 512), ml_dtypes.bfloat16)
    for par in range(2):
        tt_ = (2 * np.arange(1024, dtype=np.int64) + par)
        ph2 = ((2 * fb[:, None] + 1) * tt_[None, :]) % (2 * NF)
        a2 = ph2.astype(np.float64) * (2.0 * np.pi / (2 * NF))
        for rg, M in enumerate((np.cos(a2), np.sin(a2))):
            Mr = M.reshape(8, 128, 2, 512)
            for tile in range(2):
                inv[par, tile, rg] = Mr[:, :, tile, :].transpose(1, 0, 2).astype(ml_dtypes.bfloat16)
    c["dft_i"] = inv.reshape(8, 128, 4096)
    f32 = np.float32
    tl = np.linspace(0.0, 1.0, L, dtype=f32)[:, None]
    bands = np.linspace(1e-4, 16 - 1, 16, dtype=f32)
    ang2 = (f32(2.0 * math.pi) * np.arange(L, dtype=f32)[:, None] / f32(L) * bands).astype(f32)
    z = np.concatenate([tl, np.cos(ang2), -np.sin(ang2)], axis=-1).astype(f32)
    c["hy_zT"] = np.ascontiguousarray(z.T)
    c["hy_negt"] = np.ascontiguousarray((-tl[:, 0]).reshape(16, 128).T).astype(f32)
    lt = math.log(1e-2)
    deltas = np.abs(np.linspace(lt / 1.5, lt / 0.3, D, dtype=f32)).astype(f32)
    c["hy_delta"] = np.ascontiguousarray(np.broadcast_to(deltas.reshape(2, 1, 512), (2, 128, 512))).astype(f32)
    c["ones128"] = np.ones((128, 128), np.float32).astype(ml_dtypes.bfloat16)
    return c


_DT = {np.dtype(np.float32): F32, np.dtype(ml_dtypes.bfloat16): BF16}


def run(inputs, layers=(0, 1, 2, 3), n_cores=N_CORES, nseq=SEQ_PER_CORE, trace=False):
    inp = {k: np.asarray(v) for k, v in inputs.items()}
    shared = prep_shared(inp)
    shared.update(consts())
    shapes = {k: (v.shape, _DT[v.dtype]) for k, v in shared.items()}
    bld = Builder(shapes, list(layers), nseq)
    nc = bld.build()
    x = inp["x"].astype(np.float32)
    in_maps = []
    for c in range(n_cores):
        xs = x[c * nseq:(c + 1) * nseq]
        xl = np.ascontiguousarray(xs.reshape(nseq, L, NK, 128).transpose(0, 3, 2, 1))
        m = dict(shared)
        m["x"] = xl
        in_maps.append(m)
    res = run_bass_kernel_spmd(nc, in_maps, core_ids=list(range(n_cores)), trace=trace)
    outs = []
    for c in range(n_cores):
        y = res.results[c]["y"]
        outs.append(np.ascontiguousarray(y.transpose(0, 3, 2, 1)).reshape(nseq, L, D))
    out = np.concatenate(outs, axis=0).astype(np.float32)
    return out, res


def kernel(**inputs):
    out, _ = run(inputs)
    return out
```
